# Optimizing a Trainium2 kernel written in Bass

```python
import jax, jax.numpy as jnp
from jax import lax
import numpy as np

D_MODEL = 2048
BATCH = 4
SEQ = 4096
DEPTH = 1
DEC_BATCH = 4
DEC_SEQ = 8192
PAST_LEN = 128

GRID_W = 64
HEAD_DIM = 128
N_Q_HEADS = 8
N_KV_HEADS = 2
Q_PER_KV = N_Q_HEADS // N_KV_HEADS
ATTN_WIDTH = N_Q_HEADS * HEAD_DIM
KV_WIDTH = N_KV_HEADS * HEAD_DIM
POOL_WINDOWS = (2, 4, 8, 16)
N_POOL_GROUPS = len(POOL_WINDOWS)
POOL_GROUP_WIDTH = 256
POOL_WIDTH = N_POOL_GROUPS * POOL_GROUP_WIDTH
MIX_WIDTH = ATTN_WIDTH + POOL_WIDTH
IN_WIDTH = ATTN_WIDTH + 2 * KV_WIDTH + POOL_WIDTH
D_FF = 5632
CONV_WIDTH = 3
ROPE_THETA = 10000.0
ROPE_AXIS_DIM = HEAD_DIM // 2
Q_BLOCK = 128
EPS = 1e-6

kernel_name = "hymba_style_gqa_pool_convglu_encoder"


def rmsnorm(x, g):
    xf = x.astype(jnp.float32)
    y = xf * lax.rsqrt(jnp.mean(xf * xf, axis=-1, keepdims=True) + EPS)
    return (y * g.astype(jnp.float32)).astype(x.dtype)


def axial_rope_tables(L):
    rows = L // GRID_W
    row = jnp.repeat(jnp.arange(rows, dtype=jnp.float32), GRID_W)
    col = jnp.tile(jnp.arange(GRID_W, dtype=jnp.float32), rows)
    inv_freq = ROPE_THETA ** (-jnp.arange(0, ROPE_AXIS_DIM, 2, dtype=jnp.float32) / ROPE_AXIS_DIM)
    ang = jnp.stack([row[:, None] * inv_freq, col[:, None] * inv_freq], axis=1)
    return jnp.cos(ang), jnp.sin(ang)


def apply_axial_rope(x, cos, sin):
    B, L, H, _ = x.shape
    xf = x.astype(jnp.float32).reshape(B, L, H, 2, ROPE_AXIS_DIM)
    x1, x2 = xf[..., : ROPE_AXIS_DIM // 2], xf[..., ROPE_AXIS_DIM // 2:]
    c, s = cos[None, :, None], sin[None, :, None]
    out = jnp.concatenate([x1 * c - x2 * s, x2 * c + x1 * s], axis=-1)
    return out.reshape(B, L, H, HEAD_DIM).astype(x.dtype)


def bidir_gqa(q, k, v):
    B, L, _, _ = q.shape
    nblk = L // Q_BLOCK
    qb = q.reshape(B, nblk, Q_BLOCK, N_KV_HEADS, Q_PER_KV, HEAD_DIM).transpose(1, 0, 2, 3, 4, 5)
    scale = HEAD_DIM ** -0.5

    def one_block(q_blk):
        s = jnp.einsum('bqkgd,bskd->bkgqs', q_blk, k, preferred_element_type=jnp.float32) * scale
        p = jax.nn.softmax(s, axis=-1)
        return jnp.einsum('bkgqs,bskd->bqkgd', p.astype(v.dtype), v)

    ob = lax.map(one_block, qb)
    return ob.transpose(1, 0, 2, 3, 4, 5).reshape(B, L, ATTN_WIDTH)


def multiscale_pool(u, w_pool, pool_scale):
    B, L, _ = u.shape
    uf = u.astype(jnp.float32).reshape(B, L, N_POOL_GROUPS, POOL_GROUP_WIDTH)
    csum = jnp.concatenate([jnp.zeros((B, 1, N_POOL_GROUPS, POOL_GROUP_WIDTH), jnp.float32),
                            jnp.cumsum(uf, axis=1)], axis=1)
    t = jnp.arange(L)
    outs = []
    for g, w in enumerate(POOL_WINDOWS):
        lo = jnp.clip(t - w // 2, 0, L)
        hi = jnp.clip(t + w // 2, 0, L)
        cnt = (hi - lo).astype(jnp.float32)
        cs = csum[:, :, g]
        mean = (cs[:, hi] - cs[:, lo]) / cnt[None, :, None]
        outs.append(mean - uf[:, :, g])
    d = jnp.stack(outs, axis=2).astype(u.dtype)
    y = jnp.einsum('blgc,gcd->blgd', d, w_pool)
    return y.reshape(B, L, POOL_WIDTH) * pool_scale


def dwconv_centred(h, w, b):
    hp = jnp.pad(h, ((0, 0), (1, 1), (0, 0)))
    return hp[:, :-2] * w[0] + hp[:, 1:-1] * w[1] + hp[:, 2:] * w[2] + b


def encoder_layer(x, g_norm_mix, w_in, g_q, g_k, w_pool, pool_scale, w_out,
                  g_norm_ffn, w_up, w_conv, b_conv, w_down):
    B, L, _ = x.shape
    h = rmsnorm(x, g_norm_mix)
    z = h @ w_in
    q = z[..., :ATTN_WIDTH].reshape(B, L, N_Q_HEADS, HEAD_DIM)
    k = z[..., ATTN_WIDTH:ATTN_WIDTH + KV_WIDTH].reshape(B, L, N_KV_HEADS, HEAD_DIM)
    v = z[..., ATTN_WIDTH + KV_WIDTH:ATTN_WIDTH + 2 * KV_WIDTH].reshape(B, L, N_KV_HEADS, HEAD_DIM)
    u = z[..., ATTN_WIDTH + 2 * KV_WIDTH:]
    cos, sin = axial_rope_tables(L)
    q = apply_axial_rope(rmsnorm(q, g_q), cos, sin)
    k = apply_axial_rope(rmsnorm(k, g_k), cos, sin)
    a = bidir_gqa(q, k, v)
    m = multiscale_pool(u, w_pool, pool_scale)
    x = x + jnp.concatenate([a, m], axis=-1) @ w_out
    h = rmsnorm(x, g_norm_ffn)
    gu = h @ w_up
    gate = dwconv_centred(gu[..., :D_FF], w_conv, b_conv)
    val = gu[..., D_FF:]
    return x + (jax.nn.silu(gate) * val) @ w_down


def run_trunk(x, g_norm_mix, w_in, g_q, g_k, w_pool, pool_scale, w_out,
              g_norm_ffn, w_up, w_conv, b_conv, w_down):
    for i in range(DEPTH):
        x = encoder_layer(x, g_norm_mix[i], w_in[i], g_q[i], g_k[i], w_pool[i], pool_scale[i],
                          w_out[i], g_norm_ffn[i], w_up[i], w_conv[i], b_conv[i], w_down[i])
    return x


def setup_inputs(seed: int = 0) -> dict:
    key = jax.random.key(seed)
    ks = jax.random.split(key, 16)
    f32 = jnp.float32
    nrm = lambda k, shape, s: jax.random.normal(k, shape, f32) * s
    return {
        "x_prompt": nrm(ks[0], (BATCH, SEQ, D_MODEL), 1.0),
        "x_sample": nrm(ks[1], (DEC_BATCH, DEC_SEQ, D_MODEL), 1.0),
        "g_norm_mix": 1.0 + nrm(ks[2], (DEPTH, D_MODEL), 0.02),
        "w_in": nrm(ks[3], (DEPTH, D_MODEL, IN_WIDTH), D_MODEL ** -0.5),
        "g_q": 1.0 + nrm(ks[4], (DEPTH, HEAD_DIM), 0.02),
        "g_k": 1.0 + nrm(ks[5], (DEPTH, HEAD_DIM), 0.02),
        "w_pool": nrm(ks[6], (DEPTH, N_POOL_GROUPS, POOL_GROUP_WIDTH, POOL_GROUP_WIDTH), POOL_GROUP_WIDTH ** -0.5),
        "pool_scale": 1.0 + nrm(ks[7], (DEPTH, POOL_WIDTH), 0.02),
        "w_out": nrm(ks[8], (DEPTH, MIX_WIDTH, D_MODEL), MIX_WIDTH ** -0.5),
        "g_norm_ffn": 1.0 + nrm(ks[9], (DEPTH, D_MODEL), 0.02),
        "w_up": nrm(ks[10], (DEPTH, D_MODEL, 2 * D_FF), D_MODEL ** -0.5),
        "w_conv": nrm(ks[11], (DEPTH, CONV_WIDTH, D_FF), CONV_WIDTH ** -0.5),
        "b_conv": nrm(ks[12], (DEPTH, D_FF), 0.01),
        "w_down": nrm(ks[13], (DEPTH, D_FF, D_MODEL), D_FF ** -0.5),
    }


def reference(x_prompt, x_sample, g_norm_mix, w_in, g_q, g_k, w_pool, pool_scale, w_out,
              g_norm_ffn, w_up, w_conv, b_conv, w_down):
    y_prompt = run_trunk(x_prompt, g_norm_mix, w_in, g_q, g_k, w_pool, pool_scale, w_out,
                         g_norm_ffn, w_up, w_conv, b_conv, w_down)
    y_sample = run_trunk(x_sample, g_norm_mix, w_in, g_q, g_k, w_pool, pool_scale, w_out,
                         g_norm_ffn, w_up, w_conv, b_conv, w_down)
    return (y_prompt, y_sample)
```

```python
import numpy as np
from contextlib import ExitStack
import concourse.bass as bass
import concourse.mybir as mybir
from concourse.bass_utils import run_bass_kernel_spmd

F32 = mybir.dt.float32
BF16 = mybir.dt.bfloat16
AF = mybir.ActivationFunctionType
ALU = mybir.AluOpType
AX = mybir.AxisListType

D = 2048
NKC = 16
HD = 128
NQH = 8
NKV = 2
INW = 2560
DFF = 5632
NFC = 44
GRID_W = 64
EPS = 1e-6
SCALE = HD ** -0.5
POOL_WINDOWS = (2, 4, 8, 16)
NEG = -30000.0

OPT_E1024 = False
OPT_ACC1024 = True


class Eng:
    def __init__(self, name, h, sem):
        self.name, self.h, self.sem, self.count, self.seen = name, h, sem, 0, {}


class Buf:
    __slots__ = ("name", "w", "r", "dsem", "dcount", "dkey")

    def __init__(self, name):
        self.name = name
        self.w = {}
        self.r = {}
        self.dsem = None
        self.dcount = 0
        self.dkey = None


class KB:
    def __init__(self, nc, es):
        self.nc = nc
        self.es = es
        mk = lambda n: es.enter_context(nc.semaphore(n))
        self.pe = Eng("pe", nc.tensor, mk("s_pe"))
        self.act = Eng("act", nc.scalar, mk("s_act"))
        self.dve = Eng("dve", nc.vector, mk("s_dve"))
        self.pool = Eng("pool", nc.gpsimd, mk("s_pool"))
        self.sp = Eng("sp", nc.sync, None)
        self.engs = [self.pe, self.act, self.dve, self.pool, self.sp]
        self.free_dsems = []
        self.nd = 0
        self.dma_events = {}

    def _waits(self, eng, deps):
        for key, (sem, val) in deps.items():
            if eng is self.pe and key == "pe":
                continue
            if eng.seen.get(key, 0) < val:
                eng.h.wait_ge(sem, val)
                eng.seen[key] = val

    @staticmethod
    def _merge(deps, d):
        for k, (s, v) in d.items():
            if k not in deps or deps[k][1] < v:
                deps[k] = (s, v)

    def _deps(self, reads, writes):
        deps = {}
        for b in reads:
            self._merge(deps, b.w)
        for b in writes:
            self._merge(deps, b.w)
            self._merge(deps, b.r)
        return deps

    def _record(self, key, ev, reads, writes):
        for b in reads:
            if key not in b.r or b.r[key][1] < ev[1]:
                b.r[key] = ev
        for b in writes:
            b.w = {key: ev}
            b.r = {}

    def op(self, eng, reads, writes, fn):
        self._waits(eng, self._deps(reads, writes))
        ins = fn()
        eng.count += 1
        ins.then_inc(eng.sem, 1)
        self._record(eng.name, (eng.sem, eng.count), reads, writes)

    def _dsem(self, buf):
        if buf.dsem is None:
            if self.free_dsems:
                buf.dsem, buf.dcount, buf.dkey = self.free_dsems.pop()
            else:
                self.nd += 1
                buf.dsem = self.es.enter_context(self.nc.semaphore("s_d%d" % self.nd))
                buf.dcount = 0
                buf.dkey = "d%d" % self.nd

    def release(self, bufs):
        for b in bufs:
            if b.dsem is not None:
                self.free_dsems.append((b.dsem, b.dcount, b.dkey))
                b.dsem = None

    def dma(self, out, in_, reads, writes, sbuf, q=None, extra=None):
        q = q or self.sp
        self._dsem(sbuf)
        deps = self._deps(reads, writes)
        if sbuf.dcount > 0:
            self._merge(deps, {sbuf.dkey: (sbuf.dsem, sbuf.dcount)})
        if extra:
            self._merge(deps, extra)
        self._waits(q, deps)
        ins = q.h.dma_start(out=out, in_=in_)
        sbuf.dcount += 16
        ins.then_inc(sbuf.dsem, 16)
        ev = (sbuf.dsem, sbuf.dcount)
        self.dma_events[sbuf.dkey] = ev
        self._record(sbuf.dkey, ev, reads, writes)

    def barrier(self):
        evs = dict(self.dma_events)
        for e in self.engs:
            if e.sem is not None and e.count > 0:
                evs[e.name] = (e.sem, e.count)
        for e in self.engs:
            for key, (sem, val) in evs.items():
                if key == e.name:
                    continue
                if e.seen.get(key, 0) < val:
                    e.h.wait_ge(sem, val)
                    e.seen[key] = val


def V(t, off, dims):
    a = t[:]
    return bass.AP(a.tensor, off, [[a.ap[0][0], 128]] + [[s, c] for s, c in dims])


def build(cfg):
    phases = cfg["phases"]
    nc = bass.Bass("TRN2", target_bir_lowering=False)
    din = lambda n, shp, dt=F32: nc.dram_tensor(n, list(shp), dt, kind="ExternalInput").ap()
    dout = lambda n, shp: nc.dram_tensor(n, list(shp), F32, kind="ExternalOutput").ap()
    dbg = cfg.get("debug", False)
    dint = lambda n, shp, dt=BF16: nc.dram_tensor(n, list(shp), dt, kind=("ExternalOutput" if (dbg and n[:2] in ("QT", "U_", "H2")) else "Internal")).ap()

    P = {}
    dbg = cfg.get("debug", False)
    for ph in phases:
        n, T, nown = ph["name"], ph["T"], ph["nown"]
        NT, NQ = T + 1, nown + 2
        ph["NT"], ph["NQ"] = NT, NQ
        P[n] = dict(
            x=din("x_" + n, [NT * 128, D]),
            rope=din("rope_" + n, [NT * 128, 128]),
            kb=din("kb_" + n, [128, NT]),
            y=dout("y_" + n, [nown * 128, D]),
            QT=dint("QT_" + n, [NQ, 128, NQH * 128]),
            U=dint("U_" + n, [NQ, 128, 1024]),
            H2T=dint("H2T_" + n, [NKC, 128, NQ * 128]),
        )
        if dbg:
            P[n]["dKT"] = nc.dram_tensor("dKT_" + n, [128, NKV * NT * 128], BF16, kind="ExternalOutput").ap()
            P[n]["dV"] = nc.dram_tensor("dV_" + n, [128, NT * 256], BF16, kind="ExternalOutput").ap()
            P[n]["dmix"] = nc.dram_tensor("dmix_" + n, [NQ // 2, 128, NKC * 256], BF16, kind="ExternalOutput").ap()
    w_in = din("w_in", [D, INW]); w_out = din("w_out", [D, D]); w_up = din("w_up", [D, 2 * DFF])
    w_down = din("w_down", [DFF, D]); w_pool = din("w_pool", [1024, 256])
    gmix_d = din("gmix", [128, NKC]); gffn_d = din("gffn", [128, NKC])
    gq_d = din("gq", [128, 128]); gk_d = din("gk", [128, 128])
    pscale_d = din("pscale", [128, 8]); wconv_d = din("wconv", [128, NFC * 3]); bconv_d = din("bconv", [128, NFC])
    cmask_d = din("cmask", [128, 2]); bmat_d = din("bmat", [128, 20 * 128])
    b_in = dint("b_in", [D, INW]); b_out = dint("b_out", [D, D]); b_up = dint("b_up", [D, 2 * DFF])
    b_down = dint("b_down", [DFF, D]); b_pool = dint("b_pool", [1024, 256])

    with ExitStack() as es:
        kb = KB(nc, es)
        pe, act, dve, pool, sp = kb.pe, kb.act, kb.dve, kb.pool, kb.sp

        uid = [0]

        def sbt(st, name, shape, dt):
            uid[0] += 1
            return st.enter_context(nc.sbuf_tensor("sb%d_%s" % (uid[0], name), list(shape), dt))

        ps = es.enter_context(nc.psum_tensor("ps", [128, 8, 512], F32))
        bankB = [Buf("bank%d" % i) for i in range(8)]
        bank = lambda i: ps[:, i, :]
        psbf = lambda i, n=1: V(ps, i * 512, [(1, 512 * n)]).bitcast(BF16)

        ident = sbt(es, "ident", [128, 128], BF16); identf = sbt(es, "identf", [128, 128], F32)
        ones = sbt(es, "ones", [128, 128], BF16); onesf = sbt(es, "onesf", [128, 128], F32)
        gmix = sbt(es, "gmix", [128, NKC], F32); gffn = sbt(es, "gffn", [128, NKC], F32)
        gq = sbt(es, "gq", [128, 128], F32); gk = sbt(es, "gk", [128, 128], F32)
        pscale = sbt(es, "pscale", [128, 8], F32); wconv = sbt(es, "wconv", [128, NFC * 3], F32)
        bconv = sbt(es, "bconv", [128, NFC], F32); cmask = sbt(es, "cmask", [128, 2], F32)
        epsT = sbt(es, "epsT", [128, 1], F32); negc = sbt(es, "negc", [128, 1], F32)
        cst = sbt(es, "cst", [128, 4], F32)
        B_c = Buf("consts")
        for i, (t, d_) in enumerate([(gmix, gmix_d), (gffn, gffn_d), (gq, gq_d), (gk, gk_d), (pscale, pscale_d),
                                     (wconv, wconv_d), (bconv, bconv_d), (cmask, cmask_d)]):
            kb.dma(t[:], d_, [], [B_c], B_c)
        B_id = Buf("ident")
        kb.op(pool, [], [B_id], lambda: nc.gpsimd.memset(identf[:], 0.0))
        kb.op(pool, [], [B_id], lambda: nc.gpsimd.affine_select(out=identf[:], in_=identf[:], compare_op=ALU.not_equal,
                                                                 fill=1.0, base=0, pattern=[[-1, 128]], channel_multiplier=1))
        B_id2 = Buf("ident2")
        kb.op(dve, [B_id], [B_id2], lambda: nc.vector.tensor_copy(ident[:], identf[:]))
        kb.op(dve, [], [B_id2], lambda: nc.vector.memset(ones[:], 1.0))
        kb.op(dve, [], [B_id2], lambda: nc.vector.memset(onesf[:], 1.0))
        kb.op(dve, [], [B_id2], lambda: nc.vector.memset(epsT[:], EPS))
        gsq = sbt(es, "gsq", [128, 256], F32)
        kb.op(dve, [B_c], [B_id2], lambda: nc.vector.tensor_tensor(out=gsq[:, 0:128], in0=gq[:], in1=gq[:], op=ALU.mult))
        kb.op(dve, [B_c, B_id2], [B_id2], lambda: nc.vector.tensor_tensor(out=gsq[:, 128:256], in0=gk[:], in1=gk[:], op=ALU.mult))
        kb.op(dve, [B_id2], [B_id2], lambda: nc.vector.tensor_reduce(out=cst[:, 0:2], in_=gsq[:].rearrange("p (a b) -> p a b", b=128), axis=AX.X, op=ALU.max))
        kb.op(dve, [B_id2], [B_id2], lambda: nc.vector.tensor_tensor(out=cst[:, 2:3], in0=cst[:, 0:1], in1=cst[:, 1:2], op=ALU.mult))
        kb.op(act, [B_id2], [B_id2], lambda: nc.scalar.activation(out=cst[:, 3:4], in_=cst[:, 2:3], func=AF.Ln, scale=float(HD)))
        kb.op(act, [B_id2], [B_id2], lambda: nc.scalar.activation(out=cst[:, 2:3], in_=cst[:, 3:4], func=AF.Exp, scale=0.5))
        kb.op(dve, [B_id2], [B_id2], lambda: nc.vector.tensor_scalar(out=negc[:], in0=cst[:, 2:3], scalar1=-1.0, scalar2=None, op0=ALU.mult))
        CONST = [B_c, B_id2]

        conv_ev = {}

        def convert(name, src, dst, rows, rchunk):
            b = Buf("cv_" + name)
            kb._dsem(b)
            n = 0
            for r0 in range(0, rows, rchunk):
                r1 = min(rows, r0 + rchunk)
                nc.gpsimd.dma_start(out=dst[r0:r1, :], in_=src[r0:r1, :]).then_inc(b.dsem, 16)
                n += 1
            b.dcount += 16 * n
            conv_ev[name] = {b.dkey: (b.dsem, b.dcount)}
            kb.dma_events[b.dkey] = (b.dsem, b.dcount)

        def emit_conversions():
            convert("pool", w_pool, b_pool, 1024, 128)
            convert("out", w_out, b_out, D, 128)

        conv_tasks = []

        def plan_convert(name, src, dst, rows, rchunk):
            b = Buf("cv_" + name)
            kb._dsem(b)
            n = 0
            for r0 in range(0, rows, rchunk):
                r1 = min(rows, r0 + rchunk)
                conv_tasks.append((b, dst[r0:r1, :], src[r0:r1, :]))
                n += 1
            conv_ev[name] = {b.dkey: (b.dsem, b.dcount + 16 * n)}

        def run_conv_tasks(k):
            if not conv_tasks:
                return
            if pe.count > 0 and pool.seen.get("pe", 0) < pe.count:
                nc.gpsimd.wait_ge(pe.sem, pe.count)
                pool.seen["pe"] = pe.count
            for _ in range(min(k, len(conv_tasks))):
                b, dst, src = conv_tasks.pop(0)
                nc.gpsimd.dma_start(out=dst, in_=src).then_inc(b.dsem, 16)
                b.dcount += 16
                kb.dma_events[b.dkey] = (b.dsem, b.dcount)

        def emit_conversions_late():
            plan_convert("in", w_in, b_in, D, 128)
            plan_convert("up", w_up, b_up, D, 128)
            plan_convert("down", w_down, b_down, DFF, 128)

        def rms_rstd(eng_sq, src_ap, junk_ap, ss, rstd, n, rd, wr):
            kb.op(dve, [], wr, lambda: nc.vector.memset(ss, 0.0))
            kb.op(act, rd + wr, wr, lambda: nc.scalar.activation(out=junk_ap, in_=src_ap, func=AF.Square, accum_out=ss))
            kb.op(act, wr + CONST, wr, lambda: nc.scalar.activation(out=ss, in_=ss, func=AF.Ln, bias=epsT[:], scale=1.0 / n))
            kb.op(act, wr, wr, lambda: nc.scalar.activation(out=rstd, in_=ss, func=AF.Exp, scale=-0.5))

        for ph in phases:
            n, T, nown, NT, NQ = ph["name"], ph["T"], ph["nown"], ph["NT"], ph["NQ"]
            A = P[n]
            with ExitStack() as pst:
                KT = sbt(pst, "KT_" + n, [128, NKV, NT * 128], BF16)
                VV = sbt(pst, "V_" + n, [128, NT, 256], BF16)
                B_KT = [Buf("KT%d" % i) for i in range(NT)]
                B_V = [Buf("V%d" % i) for i in range(NT)]
                with ExitStack() as st:
                    win = sbt(st, "win", [128, NKC, INW], BF16)
                    xt = sbt(st, "xt", [128, D], F32)
                    xh = sbt(st, "xh", [128, D], BF16)
                    hT = [sbt(st, "hT%d" % i, [128, NKC, 128], BF16) for i in range(2)]
                    z = [sbt(st, "z%d" % i, [128, 1280], F32) for i in range(2)]
                    qn = [sbt(st, "qn%d" % i, [128, 1280], F32) for i in range(2)]
                    t1 = sbt(st, "t1", [128, 640], F32); t2 = sbt(st, "t2", [128, 640], F32)
                    qkb = sbt(st, "qkb", [128, 1280], BF16)
                    ub = [sbt(st, "ub%d" % i, [128, 1024], BF16) for i in range(2)]
                    qTs = [sbt(st, "qTs%d" % i, [128, NQH * 128], BF16) for i in range(2)]
                    rt = [sbt(st, "rt%d" % i, [128, 128], F32) for i in range(2)]
                    stF = sbt(st, "stF", [128, 4], F32)
                    stB = [sbt(st, "stB%d" % i, [128, 32], F32) for i in range(2)]
                    junk = sbt(st, "junk", [128, 128], BF16)
                    B_win = Buf("win"); B_xt = Buf("xt"); B_xh = Buf("xh"); B_hT = [Buf("hT0"), Buf("hT1")]
                    B_z = [Buf("z0"), Buf("z1")]; B_qn = [Buf("qn0"), Buf("qn1")]; B_t1 = Buf("t1"); B_t2 = Buf("t2")
                    B_qkb = Buf("qkb"); B_ub = [Buf("ub0"), Buf("ub1")]; B_qTs = [Buf("qTs0"), Buf("qTs1")]
                    B_rt = [Buf("rt0"), Buf("rt1")]; B_stF = Buf("stF"); B_stB = [Buf("stB0"), Buf("stB1")]; B_junk = Buf("junk")
                    p1bufs = [B_win, B_xt, B_ub[0], B_ub[1], B_qTs[0], B_qTs[1], B_rt[0], B_rt[1]]
                    CB = [1024, 0, 512, 1536, 2048]
                    B_wc = {c0: Buf("win_c%d" % c0) for c0 in CB}
                    p1bufs = p1bufs + list(B_wc.values())
                    if ph is phases[0]:
                        fsrc = w_in.rearrange("(kc p) n -> p kc n", p=128)
                        for c0 in CB:
                            kb.dma(win[:, :, c0:c0 + 512], fsrc[:, :, c0:c0 + 512], [], [B_wc[c0]], B_wc[c0], q=pool)
                        emit_conversions()
                    else:
                        bsrc = b_in.rearrange("(kc p) n -> p kc n", p=128)
                        for c0 in CB:
                            kb.dma(win[:, :, c0:c0 + 512], bsrc[:, :, c0:c0 + 512], [], [B_wc[c0]], B_wc[c0], extra=conv_ev["in"])

                    def load_x(i):
                        kb.dma(xt[:], A["x"][i * 128:(i + 1) * 128, :], [], [B_xt], B_xt)

                    def front_a(i, hb, inext):
                        isq = i < NQ
                        kb.dma(rt[hb][:], A["rope"][i * 128:(i + 1) * 128, :], [], [B_rt[hb]], B_rt[hb])
                        rms_rstd(act, xt[:], xh[:], stF[:, 0:1], stF[:, 1:2], D, [B_xt], [B_stF, B_xh])
                        kb.op(dve, [B_xt, B_stF], [B_xh], lambda: nc.vector.tensor_scalar(out=xh[:], in0=xt[:], scalar1=stF[:, 1:2], scalar2=None, op0=ALU.mult))
                        if inext is not None:
                            load_x(inext)

                    def front_a2(i, hb):
                        isq = i < NQ

                        def tr16():
                            o = psbf(0, 2)
                            for kc in range(NKC):
                                ins = nc.tensor.transpose(out=o[:, kc * 128:(kc + 1) * 128], in_=xh[:, kc * 128:(kc + 1) * 128], identity=ident[:])
                            return ins
                        kb.op(pe, [B_xh] + CONST, [bankB[0], bankB[1]], tr16)
                        kb.op(dve, [bankB[0], bankB[1]] + CONST, [B_hT[hb]], lambda: nc.vector.tensor_tensor(
                            out=hT[hb][:], in0=psbf(0, 2).rearrange("p (a b) -> p a b", b=128),
                            in1=gmix[:].unsqueeze(2).broadcast_to([128, NKC, 128]), op=ALU.mult))

                        def mm_in(b_, c0):
                            def f():
                                for kc in range(NKC):
                                    ins = nc.tensor.matmul(bank(b_), lhsT=hT[hb][:, kc, :], rhs=win[:, kc, c0:c0 + 512], start=(kc == 0), stop=(kc == NKC - 1))
                                return ins
                            return f
                        kb.op(pe, [B_hT[hb], B_wc[1024]], [bankB[2]], mm_in(2, 1024))
                        if isq:
                            kb.op(pe, [B_hT[hb], B_wc[0]], [bankB[3]], mm_in(3, 0))
                            kb.op(pe, [B_hT[hb], B_wc[512]], [bankB[4]], mm_in(4, 512))
                            kb.op(pe, [B_hT[hb], B_wc[1536]], [bankB[5]], mm_in(5, 1536))
                            kb.op(pe, [B_hT[hb], B_wc[2048]], [bankB[6]], mm_in(6, 2048))

                    def front_b(i, hb):
                        isq = i < NQ
                        kb.op(act, [bankB[2]], [B_V[i]], lambda: nc.scalar.copy(out=VV[:, i, :], in_=ps[:, 2, 256:512]))
                        kb.op(act, [bankB[2]], [B_z[hb]], lambda: nc.scalar.copy(out=z[hb][:, 1024:1280], in_=ps[:, 2, 0:256]))
                        if isq:
                            kb.op(act, [bankB[3], bankB[4]], [B_z[hb]], lambda: nc.scalar.copy(out=z[hb][:, 0:1024], in_=V(ps, 3 * 512, [(1, 1024)])))
                            kb.op(act, [bankB[5], bankB[6]], [B_ub[hb]], lambda: nc.scalar.copy(out=ub[hb][:], in_=V(ps, 5 * 512, [(1, 1024)])))
                            kb.dma(A["U"][i], ub[hb][:], [B_ub[hb]], [], B_ub[hb])

                    def back(i, hb):
                        isq = i < NQ
                        h0 = 0 if isq else 8
                        H = 10 - h0
                        zz, qq, sB = z[hb], qn[hb], stB[hb]
                        zs = zz[:, h0 * 128:1280]; qs = qq[:, h0 * 128:1280]
                        kb.op(dve, [], [B_stB[hb]], lambda: nc.vector.memset(sB[:, 0:16], 0.0))
                        for h in range(h0, 10):
                            kb.op(act, [B_z[hb], B_stB[hb]], [B_stB[hb], B_junk], lambda h=h: nc.scalar.activation(
                                out=junk[:], in_=zz[:, h * 128:(h + 1) * 128], func=AF.Square, accum_out=sB[:, h:h + 1]))
                        kb.op(act, [B_stB[hb]] + CONST, [B_stB[hb]], lambda: nc.scalar.activation(out=sB[:, h0:10], in_=sB[:, h0:10], func=AF.Ln, bias=epsT[:], scale=1.0 / HD))
                        kb.op(act, [B_stB[hb]], [B_stB[hb]], lambda: nc.scalar.activation(out=sB[:, 16 + h0:26], in_=sB[:, h0:10], func=AF.Exp, scale=-0.5))
                        kb.op(dve, [B_z[hb], B_stB[hb]], [B_qn[hb]], lambda: nc.vector.tensor_tensor(
                            out=qs.rearrange("p (h d) -> p h d", d=128), in0=zs.rearrange("p (h d) -> p h d", d=128),
                            in1=sB[:, 16 + h0:26].unsqueeze(2).broadcast_to([128, H, 128]), op=ALU.mult))
                        if isq:
                            kb.op(dve, [B_qn[hb]] + CONST, [B_qn[hb]], lambda: nc.vector.tensor_tensor(
                                out=qq[:, 0:1024].rearrange("p (h d) -> p h d", d=128), in0=qq[:, 0:1024].rearrange("p (h d) -> p h d", d=128),
                                in1=gq[:].unsqueeze(1).broadcast_to([128, 8, 128]), op=ALU.mult))
                        kb.op(dve, [B_qn[hb]] + CONST, [B_qn[hb]], lambda: nc.vector.tensor_tensor(
                            out=qq[:, 1024:1280].rearrange("p (h d) -> p h d", d=128), in0=qq[:, 1024:1280].rearrange("p (h d) -> p h d", d=128),
                            in1=gk[:].unsqueeze(1).broadcast_to([128, 2, 128]), op=ALU.mult))
                        x1 = V(qq, h0 * 128, [(128, H), (64, 2), (1, 32)]); x2 = V(qq, h0 * 128 + 32, [(128, H), (64, 2), (1, 32)])
                        o1 = V(qkb, h0 * 128, [(128, H), (64, 2), (1, 32)]); o2 = V(qkb, h0 * 128 + 32, [(128, H), (64, 2), (1, 32)])
                        cc = V(rt[hb], 0, [(0, H), (32, 2), (1, 32)]); ss_ = V(rt[hb], 64, [(0, H), (32, 2), (1, 32)])
                        ta = V(t1, 0, [(64, H), (32, 2), (1, 32)]); tb = V(t2, 0, [(64, H), (32, 2), (1, 32)])
                        kb.op(dve, [B_qn[hb], B_rt[hb]], [B_t1], lambda: nc.vector.tensor_tensor(out=ta, in0=x1, in1=cc, op=ALU.mult))
                        kb.op(dve, [B_qn[hb], B_rt[hb]], [B_t2], lambda: nc.vector.tensor_tensor(out=tb, in0=x2, in1=ss_, op=ALU.mult))
                        kb.op(dve, [B_t1, B_t2], [B_qkb], lambda: nc.vector.tensor_tensor(out=o1, in0=ta, in1=tb, op=ALU.subtract))
                        kb.op(dve, [B_qn[hb], B_rt[hb], B_qkb], [B_t1], lambda: nc.vector.tensor_tensor(out=ta, in0=x2, in1=cc, op=ALU.mult))
                        kb.op(dve, [B_qn[hb], B_rt[hb], B_qkb], [B_t2], lambda: nc.vector.tensor_tensor(out=tb, in0=x1, in1=ss_, op=ALU.mult))
                        kb.op(dve, [B_t1, B_t2, B_qkb], [B_qkb], lambda: nc.vector.tensor_tensor(out=o2, in0=ta, in1=tb, op=ALU.add))

                        def trk():
                            o = psbf(7)
                            for hk in range(2):
                                ins = nc.tensor.transpose(out=o[:, hk * 128:(hk + 1) * 128], in_=qkb[:, (8 + hk) * 128:(9 + hk) * 128], identity=ident[:])
                            return ins
                        kb.op(pe, [B_qkb] + CONST, [bankB[7]], trk)
                        kb.op(act, [bankB[7]], [B_KT[i]], lambda: nc.scalar.copy(
                            out=V(KT, i * 128, [(NT * 128, 2), (1, 128)]), in_=psbf(7)[:, 0:256].rearrange("p (h t) -> p h t", t=128)))
                        if isq:
                            def trq():
                                o = psbf(7)
                                for hq in range(8):
                                    ins = nc.tensor.transpose(out=o[:, hq * 128:(hq + 1) * 128], in_=qkb[:, hq * 128:(hq + 1) * 128], identity=ident[:])
                                return ins
                            kb.op(pe, [B_qkb] + CONST, [bankB[7]], trq)
                            kb.op(dve, [bankB[7]], [B_qTs[hb]], lambda: nc.vector.tensor_copy(qTs[hb][:], psbf(7)))
                            kb.dma(A["QT"][i], qTs[hb][:], [B_qTs[hb]], [], B_qTs[hb])

                    konly = list(range(NQ, NT)); qside = list(range(NQ))
                    nlead = min(len(konly), 8 if ph is phases[0] else 1)
                    order = konly[:nlead]; rest_k = konly[nlead:]
                    for qi in qside:
                        order.append(qi)
                        if rest_k:
                            order.append(rest_k.pop(0))
                    order += rest_k
                    assert sorted(order) == list(range(NT))
                    nxt = lambda p: order[p + 1] if p + 1 < NT else None
                    load_x(order[0])
                    front_a(order[0], 0, nxt(0))
                    front_a2(order[0], 0)
                    for p in range(NT):
                        if p + 1 < NT:
                            front_a(order[p + 1], (p + 1) % 2, nxt(p + 1))
                        front_b(order[p], p % 2)
                        if p + 1 < NT:
                            front_a2(order[p + 1], (p + 1) % 2)
                        back(order[p], p % 2)
                    if dbg:
                        B_dbg = Buf("dbg")
                        kb.dma(A["dKT"], KT[:].rearrange("p a b -> p (a b)"), B_KT, [], B_dbg)
                        kb.dma(A["dV"], VV[:].rearrange("p a b -> p (a b)"), B_V, [], B_dbg)
                    kb.barrier()
                    kb.release(p1bufs)
                    if ph is phases[0]:
                        emit_conversions_late()

                with ExitStack() as st:
                    wo = [sbt(st, "wo%d" % i, [128, NKC, 512], BF16) for i in range(2)]
                    QTg = [sbt(st, "QTg%d" % i, [128, NQH, 256], BF16) for i in range(2)]
                    ug = [sbt(st, "ug%d" % i, [128, 4, 1024], BF16) for i in range(2)]
                    PT = [sbt(st, "PT%d" % i, [128, 1024], BF16) for i in range(4)]
                    rl = sbt(st, "rl", [128, 1024], F32); Lacc = sbt(st, "Lacc", [128, 1024], F32)
                    B_acc = [Buf("acc0"), Buf("acc1")]
                    mixT = sbt(st, "mixT", [128, NKC, 256], BF16)
                    dT = sbt(st, "dT", [128, 8, 256], BF16)
                    xm = [sbt(st, "xm%d" % i, [128, D], F32) for i in range(2)]
                    sqj = sbt(st, "sqj2", [128, D], BF16); h2 = sbt(st, "h2", [128, D], BF16)
                    h2T = [sbt(st, "h2T%d" % i, [128, NKC, 128], BF16) for i in range(2)]
                    kbm = sbt(st, "kbm", [128, NT], F32); kbs = sbt(st, "kbs", [128, NT], F32)
                    bmf = sbt(st, "bmf", [128, 20 * 128], F32); bmat = sbt(st, "bmat", [128, 20, 128], BF16)
                    wpool = sbt(st, "wpool", [128, 8, 256], BF16)
                    stt = sbt(st, "stt2", [128, 8], F32)
                    B_wo = [Buf("wo0"), Buf("wo1")]; B_QTg = [Buf("QTg0"), Buf("QTg1")]; B_ug = [Buf("ug0"), Buf("ug1")]
                    B_PT = [[Buf("PT%d_%d" % (i, j)) for j in range(2)] for i in range(4)]; B_rl = Buf("rl"); B_mixT = Buf("mixT"); B_dT = Buf("dT")
                    B_xm = [Buf("xm0"), Buf("xm1")]; B_sqj = Buf("sqj"); B_h2 = Buf("h2"); B_h2T = [Buf("h2T0"), Buf("h2T1")]
                    B_kb = Buf("kb"); B_bm = Buf("bm"); B_wp = Buf("wpool"); B_st = Buf("st")
                    p2bufs = B_wo + B_QTg + B_ug + B_xm + B_h2T + [B_kb, B_bm, B_wp]
                    kb.dma(kbm[:], A["kb"], [], [B_kb], B_kb)
                    kb.op(dve, [B_kb] + CONST, [B_kb], lambda: nc.vector.tensor_scalar(out=kbs[:], in0=kbm[:], scalar1=negc[:], scalar2=None, op0=ALU.add))
                    kb.dma(bmf[:], bmat_d, [], [B_bm], B_bm)
                    kb.op(dve, [B_bm], [B_bm], lambda: nc.vector.tensor_copy(bmat[:].rearrange("p a b -> p (a b)"), bmf[:]))
                    kb.dma(wpool[:], b_pool.rearrange("(a p) n -> p a n", p=128), [], [B_wp], B_wp, extra=conv_ev["pool"])
                    NG = NQ // 2
                    wcnt = [0]

                    def load_group(g):
                        gb = g % 2
                        for tl in range(2):
                            kb.dma(QTg[gb][:, :, tl * 128:(tl + 1) * 128], A["QT"][2 * g + tl].rearrange("p (h t) -> p h t", t=128), [], [B_QTg[gb]], B_QTg[gb])
                        lo, hi = max(2 * g - 1, 0), min(2 * g + 2, NQ - 1)
                        kb.dma(ug[gb][:, 0:hi - lo + 1, :], A["U"][lo:hi + 1].rearrange("a p n -> p a n"), [], [B_ug[gb]], B_ug[gb])
                    load_group(0)
                    for g in range(NG):
                        gb = g % 2
                        for tl in range(2):
                            ti = 2 * g + tl
                            kb.dma(xm[tl][:], A["x"][ti * 128:(ti + 1) * 128, :], [], [B_xm[tl]], B_xm[tl])
                        bsrc = b_out.rearrange("(kc p) n -> p kc n", p=128)

                        def wo_dma(nn):
                            kb.dma(wo[nn % 2][:], bsrc[:, :, nn * 512:(nn + 1) * 512], [], [B_wo[nn % 2]], B_wo[nn % 2], extra=conv_ev["out"])
                        wo_dma(0)
                        wo_dma(1)
                        if conv_tasks:
                            run_conv_tasks(-(-76 // max(NG - 1, 1)) if g + 1 < NG else len(conv_tasks))
                        for kvh in range(NKV):
                            SB = [(0, 1), (2, 3), (6, 7)]

                            def s_step(kt):
                                b0, b1 = SB[kt % 3]

                                def f():
                                    for hp, bk in enumerate((b0, b1)):
                                        ins = nc.tensor.matmul(bank(bk), lhsT=KT[:, kvh, kt * 128:(kt + 1) * 128],
                                                               rhs=V(QTg[gb], (kvh * 4 + hp * 2) * 256, [(1, 512)]), start=True, stop=True)
                                    return ins
                                kb.op(pe, [B_KT[kt], B_QTg[gb]], [bankB[b0], bankB[b1]], f)

                            def e_step(kt):
                                r = kt % 4
                                for hp, bk in enumerate(SB[kt % 3]):
                                    kb.op(act, [bankB[bk], B_kb], [B_PT[r][hp]], lambda hp=hp, bk=bk: nc.scalar.activation(
                                        out=PT[r][:, hp * 512:(hp + 1) * 512], in_=bank(bk), func=AF.Exp, bias=kbs[:, kt:kt + 1], scale=SCALE))

                            def pv_step(kt):
                                r = kt % 4

                                def f():
                                    for hp in range(2):
                                        ins = nc.tensor.matmul(bank(4 + hp), lhsT=VV[:, kt, kvh * 128:(kvh + 1) * 128],
                                                               rhs=PT[r][:, hp * 512:(hp + 1) * 512], start=(kt == 0), stop=(kt == NT - 1))
                                    return ins
                                kb.op(pe, B_PT[r] + [B_V[kt]], [bankB[4], bankB[5]], f)
                                if kt == 0:
                                    kb.op(dve, B_PT[r], [B_acc[0]], lambda: nc.vector.tensor_copy(Lacc[:], PT[r][:]))
                                else:
                                    kb.op(dve, B_PT[r] + [B_acc[0]], [B_acc[0]], lambda: nc.vector.tensor_tensor(out=Lacc[:], in0=Lacc[:], in1=PT[r][:], op=ALU.add))
                            s_step(0)
                            if NT > 1:
                                s_step(1)
                            for kt in range(NT):
                                if kt + 2 < NT:
                                    s_step(kt + 2)
                                e_step(kt)
                                pv_step(kt)

                            def l_mm():
                                for hp in range(2):
                                    ins = nc.tensor.matmul(bank(6 + hp), lhsT=onesf[:], rhs=Lacc[:, hp * 512:(hp + 1) * 512], start=True, stop=True)
                                return ins
                            kb.op(pe, B_acc + CONST, [bankB[6], bankB[7]], l_mm)
                            for hp in range(2):
                                kb.op(act, [bankB[6 + hp]], [B_rl], lambda hp=hp: nc.scalar.activation(out=rl[:, hp * 512:(hp + 1) * 512], in_=bank(6 + hp), func=AF.Ln))
                            for hp in range(2):
                                kb.op(act, [B_rl], [B_rl], lambda hp=hp: nc.scalar.activation(out=rl[:, hp * 512:(hp + 1) * 512], in_=rl[:, hp * 512:(hp + 1) * 512], func=AF.Exp, scale=-1.0))
                            for hp in range(2):
                                kb.op(dve, [bankB[4 + hp], B_rl], [B_mixT], lambda hp=hp: nc.vector.tensor_tensor(
                                    out=V(mixT, (kvh * 4 + hp * 2) * 256, [(1, 512)]), in0=bank(4 + hp), in1=rl[:, hp * 512:(hp + 1) * 512], op=ALU.mult))
                        lo = max(2 * g - 1, 0)

                        def pool_mm():
                            for cc in range(8):
                                g4 = cc // 2
                                for tl in range(2):
                                    ti = 2 * g + tl
                                    cur = 8 + g4 if ti == 1 else (16 + g4 if ti == NQ - 2 else 12 + g4)
                                    srcs = [(ti, cur)]
                                    if ti - 1 >= 0:
                                        srcs.append((ti - 1, g4))
                                    if ti + 1 <= NQ - 1:
                                        srcs.append((ti + 1, 4 + g4))
                                    for si, (tsrc, bi) in enumerate(srcs):
                                        ins = nc.tensor.matmul(ps[:, cc // 2, (cc % 2) * 256 + tl * 128:(cc % 2) * 256 + (tl + 1) * 128],
                                                               lhsT=ug[gb][:, tsrc - lo, cc * 128:(cc + 1) * 128], rhs=bmat[:, bi, :],
                                                               start=(si == 0), stop=(si == len(srcs) - 1))
                            return ins
                        kb.op(pe, [B_ug[gb], B_bm], [bankB[0], bankB[1], bankB[2], bankB[3]], pool_mm)
                        for b_ in range(4):
                            e_ = act if b_ % 2 == 0 else dve
                            if e_ is act:
                                kb.op(act, [bankB[b_]], [B_dT], lambda b_=b_: nc.scalar.copy(out=V(dT, b_ * 512, [(1, 512)]), in_=bank(b_)))
                            else:
                                kb.op(dve, [bankB[b_]], [B_dT], lambda b_=b_: nc.vector.tensor_copy(V(dT, b_ * 512, [(1, 512)]), bank(b_)))

                        def pool_lin():
                            for jc in range(8):
                                g4, dc = jc // 2, jc % 2
                                for k2 in range(2):
                                    ins = nc.tensor.matmul(ps[:, 4 + jc // 2, (jc % 2) * 256:(jc % 2 + 1) * 256], lhsT=wpool[:, g4 * 2 + k2, dc * 128:(dc + 1) * 128],
                                                           rhs=dT[:, g4 * 2 + k2, :], start=(k2 == 0), stop=(k2 == 1))
                            return ins
                        kb.op(pe, [B_dT, B_wp], [bankB[4], bankB[5], bankB[6], bankB[7]], pool_lin)
                        B_mx8 = [Buf("mx%d" % jc) for jc in range(8)]
                        for jc in range(8):
                            if True:
                                kb.op(dve, [bankB[4 + jc // 2], B_mixT] + CONST, [B_mx8[jc]], lambda jc=jc: nc.vector.tensor_scalar(
                                    out=mixT[:, 8 + jc, :], in0=ps[:, 4 + jc // 2, (jc % 2) * 256:(jc % 2 + 1) * 256], scalar1=pscale[:, jc:jc + 1], scalar2=None, op0=ALU.mult))
                            else:
                                kb.op(act, [bankB[4 + jc // 2], B_mixT] + CONST, [B_mx8[jc]], lambda jc=jc: nc.scalar.activation(
                                    out=mixT[:, 8 + jc, :], in_=ps[:, 4 + jc // 2, (jc % 2) * 256:(jc % 2 + 1) * 256], func=AF.Copy, scale=pscale[:, jc:jc + 1]))
                        kb.op(dve, B_mx8, [B_mixT], lambda: nc.vector.memset(stt[:, 4:5], 0.0))
                        if dbg:
                            kb.dma(A["dmix"][g], mixT[:].rearrange("p a b -> p (a b)"), [B_mixT], [], B_mixT)
                        if g + 1 < NG:
                            load_group(g + 1)
                        for nn in range(4):
                            wb = nn % 2
                            if nn >= 2:
                                wo_dma(nn)
                            for tl in range(2):
                                ob = (nn * 2 + tl) % 4

                                def mm_o(ob=ob, tl=tl, wb=wb):
                                    for kc in range(NKC):
                                        ins = nc.tensor.matmul(bank(ob), lhsT=mixT[:, kc, tl * 128:(tl + 1) * 128], rhs=wo[wb][:, kc, :], start=(kc == 0), stop=(kc == NKC - 1))
                                    return ins
                                kb.op(pe, [B_mixT, B_wo[wb]], [bankB[ob]], mm_o)
                                kb.op(dve, [bankB[ob], B_xm[tl]], [B_xm[tl]], lambda ob=ob, tl=tl, nn=nn: nc.vector.tensor_tensor(
                                    out=xm[tl][:, nn * 512:(nn + 1) * 512], in0=bank(ob), in1=xm[tl][:, nn * 512:(nn + 1) * 512], op=ALU.add))
                        h2x = [h2, sqj]
                        B_h2x = [B_h2, B_sqj]
                        B_st2 = [Buf("st2_0"), Buf("st2_1")]
                        for tl in range(2):
                            rms_rstd(act, xm[tl][:], h2T[tl][:].rearrange("p a b -> p (a b)"), stt[:, 2 * tl:2 * tl + 1], stt[:, 2 * tl + 1:2 * tl + 2], D,
                                     [B_xm[tl]], [B_st2[tl], B_h2T[tl]])
                            kb.op(dve, [B_xm[tl], B_st2[tl]], [B_h2x[tl]], lambda tl=tl: nc.vector.tensor_scalar(
                                out=h2x[tl][:], in0=xm[tl][:], scalar1=stt[:, 2 * tl + 1:2 * tl + 2], scalar2=None, op0=ALU.mult))
                        for tl in range(2):
                            ti = 2 * g + tl

                            def tr2(tl=tl):
                                o = psbf(4 + 2 * tl, 2)
                                for kc in range(NKC):
                                    ins = nc.tensor.transpose(out=o[:, kc * 128:(kc + 1) * 128], in_=h2x[tl][:, kc * 128:(kc + 1) * 128], identity=ident[:])
                                return ins
                            kb.op(pe, [B_h2x[tl]] + CONST, [bankB[4 + 2 * tl], bankB[5 + 2 * tl]], tr2)
                            kb.op(dve, [bankB[4 + 2 * tl], bankB[5 + 2 * tl]] + CONST, [B_h2T[tl]], lambda tl=tl: nc.vector.tensor_tensor(
                                out=h2T[tl][:], in0=psbf(4 + 2 * tl, 2).rearrange("p (a b) -> p a b", b=128),
                                in1=gffn[:].unsqueeze(2).broadcast_to([128, NKC, 128]), op=ALU.mult))
                            kb.dma(A["H2T"][:, :, ti * 128:(ti + 1) * 128].rearrange("a p t -> p a t"), h2T[tl][:], [B_h2T[tl]], [], B_h2T[tl])
                            if 1 <= ti <= nown:
                                kb.dma(A["y"][(ti - 1) * 128:ti * 128, :], xm[tl][:], [B_xm[tl]], [], B_xm[tl])
                    kb.barrier()
                    kb.release(p2bufs)

        with ExitStack() as st:
            h2w = [sbt(st, "h2w%d" % i, [128, NKC, 514], BF16) for i in range(2)]
            actT = sbt(st, "actT", [128, NFC, 512], BF16)
            wu = [sbt(st, "wu%d" % i, [128, NKC, 512], BF16) for i in range(3)]
            wd = [sbt(st, "wd%d" % i, [128, 4, 512], BF16) for i in range(6)]
            G = [sbt(st, "G%d" % i, [128, 514], F32) for i in range(2)]
            T1 = [sbt(st, "T1_%d" % i, [128, 512], F32) for i in range(2)]
            T2 = [sbt(st, "T2_%d" % i, [128, 512], F32) for i in range(2)]
            SS = [sbt(st, "SS%d" % i, [128, 512], F32) for i in range(2)]
            xo = [sbt(st, "xo%d" % i, [128, D], F32) for i in range(4)]
            m2 = sbt(st, "m2", [128, 2], F32)
            B_h2w = [Buf("h2w0"), Buf("h2w1")]; B_actT = Buf("actT"); B_wu = [Buf("wu%d" % i) for i in range(3)]
            B_wd = [Buf("wd%d" % i) for i in range(6)]; B_G = [Buf("G0"), Buf("G1")]; B_T1 = [Buf("T10"), Buf("T11")]
            B_T2 = [Buf("T20"), Buf("T21")]; B_SS = [Buf("SS0"), Buf("SS1")]; B_xo = [Buf("xo%d" % i) for i in range(4)]
            B_m2 = Buf("m2")
            groups = []
            for ph in phases:
                ng = ph["nown"] // 4
                for j in range(ng):
                    groups.append((ph, j, ng))

            def load_h2w(gi):
                ph, j, ng = groups[gi]
                c0 = 127 + 512 * j
                kb.dma(h2w[gi % 2][:], P[ph["name"]]["H2T"][:, :, c0:c0 + 514].rearrange("a p t -> p a t"), [], [B_h2w[gi % 2]], B_h2w[gi % 2])
            load_h2w(0)
            ucnt = 0
            dcnt = 0
            fcnt = 0
            upsrc = b_up.rearrange("(kc p) n -> p kc n", p=128)
            wu_next = [0]

            def wu_ensure(u_hi):
                u_hi = min(u_hi, len(groups) * 22 - 1)
                while wu_next[0] <= u_hi:
                    u = wu_next[0]
                    fq_, kind = (u % 22) // 2, u % 2
                    c0 = kind * DFF + fq_ * 512
                    kb.dma(wu[u % 3][:], upsrc[:, :, c0:c0 + 512], [], [B_wu[u % 3]], B_wu[u % 3], extra=conv_ev["up"])
                    wu_next[0] += 1
            for gi, (ph, j, ng) in enumerate(groups):
                A = P[ph["name"]]
                hb = gi % 2
                kb.op(dve, [], [B_m2], lambda: nc.vector.memset(m2[:], 1.0))
                if j == 0:
                    kb.op(dve, CONST + [B_m2], [B_m2], lambda: nc.vector.tensor_copy(m2[:, 0:1], cmask[:, 0:1]))
                if j == ng - 1:
                    kb.op(dve, CONST + [B_m2], [B_m2], lambda: nc.vector.tensor_copy(m2[:, 1:2], cmask[:, 1:2]))
                for tl in range(4):
                    r0 = (4 * j + tl) * 128
                    kb.dma(xo[tl][:], A["y"][r0:r0 + 128, :], [], [B_xo[tl]], B_xo[tl])
                for fq in range(11):
                    gs = (gi * 22 + 2 * fq) % 3
                    vs = (gi * 22 + 2 * fq + 1) % 3
                    wu_ensure(gi * 22 + 2 * fq + 1)
                    for f4 in range(4):
                        fc = fq * 4 + f4
                        r = fcnt % 2; fcnt += 1
                        gbk, vbk = r, 2 + r

                        def mm_up(sl, bk, edge):
                            def f():
                                for kc in range(NKC):
                                    ins = nc.tensor.matmul(bank(bk), lhsT=wu[sl][:, kc, f4 * 128:(f4 + 1) * 128], rhs=h2w[hb][:, kc, 1:513], start=(kc == 0), stop=(kc == NKC - 1))
                                if edge:
                                    for kc in range(NKC):
                                        ins = nc.tensor.matmul(ps[:, 4, r * 2:r * 2 + 2], lhsT=wu[sl][:, kc, f4 * 128:(f4 + 1) * 128],
                                                               rhs=V(h2w[hb], kc * 514, [(513, 2)]), start=(kc == 0), stop=(kc == NKC - 1))
                                return ins
                            return f
                        kb.op(pe, [B_wu[gs], B_h2w[hb]], [bankB[gbk], bankB[4]], mm_up(gs, gbk, True))
                        kb.op(pe, [B_wu[vs], B_h2w[hb]], [bankB[vbk]], mm_up(vs, vbk, False))
                        w0 = wconv[:, fc * 3:fc * 3 + 1]; w1 = wconv[:, fc * 3 + 1:fc * 3 + 2]; w2 = wconv[:, fc * 3 + 2:fc * 3 + 3]
                        bb = bconv[:, fc:fc + 1]
                        kb.op(act, [bankB[gbk]], [B_G[r]], lambda: nc.scalar.copy(out=G[r][:, 1:513], in_=bank(gbk)))
                        kb.op(dve, [bankB[4], B_m2, B_G[r]], [B_G[r]], lambda: nc.vector.tensor_tensor(
                            out=V(G[r], 0, [(513, 2)]), in0=ps[:, 4, r * 2:r * 2 + 2], in1=m2[:], op=ALU.mult))
                        kb.op(dve, [B_G[r]] + CONST, [B_T1[r]], lambda: nc.vector.tensor_scalar(
                            out=T1[r][:], in0=G[r][:, 1:513], scalar1=w1, scalar2=bb, op0=ALU.mult, op1=ALU.add))
                        kb.op(dve, [B_G[r], B_T1[r]] + CONST, [B_T2[r]], lambda: nc.vector.scalar_tensor_tensor(
                            out=T2[r][:], in0=G[r][:, 0:512], scalar=w0, in1=T1[r][:], op0=ALU.mult, op1=ALU.add))
                        kb.op(dve, [B_G[r], B_T2[r]] + CONST, [B_T1[r]], lambda: nc.vector.scalar_tensor_tensor(
                            out=T1[r][:], in0=G[r][:, 2:514], scalar=w2, in1=T2[r][:], op0=ALU.mult, op1=ALU.add))
                        kb.op(act, [B_T1[r]], [B_SS[r]], lambda: nc.scalar.activation(out=SS[r][:], in_=T1[r][:], func=AF.Silu))
                        kb.op(dve, [B_SS[r], bankB[vbk]], [B_actT], lambda: nc.vector.tensor_tensor(
                            out=actT[:, fc, :], in0=bank(vbk), in1=SS[r][:], op=ALU.mult))
                if gi + 1 < len(groups):
                    load_h2w(gi + 1)
                    wu_ensure(gi * 22 + 24)
                for nn in range(4):
                    for ku in range(11):
                        ds = dcnt % 6; dcnt += 1
                        kb.dma(wd[ds][:], b_down[ku * 512:(ku + 1) * 512, nn * 512:(nn + 1) * 512].rearrange("(a p) n -> p a n", p=128),
                               [], [B_wd[ds]], B_wd[ds], extra=conv_ev["down"])

                        def mm_dn(ds=ds, ku=ku, nn=nn):
                            for tl in range(4):
                                for kk in range(4):
                                    ins = nc.tensor.matmul(bank((nn % 2) * 4 + tl), lhsT=actT[:, ku * 4 + kk, tl * 128:(tl + 1) * 128], rhs=wd[ds][:, kk, :],
                                                           start=(ku == 0 and kk == 0), stop=(ku == 10 and kk == 3))
                            return ins
                        kb.op(pe, [B_actT, B_wd[ds]], [bankB[(nn % 2) * 4 + tl] for tl in range(4)], mm_dn)
                    for tl in range(4):
                        bk = (nn % 2) * 4 + tl
                        kb.op(dve, [bankB[bk], B_xo[tl]], [B_xo[tl]], lambda bk=bk, tl=tl, nn=nn: nc.vector.tensor_tensor(
                            out=xo[tl][:, nn * 512:(nn + 1) * 512], in0=bank(bk), in1=xo[tl][:, nn * 512:(nn + 1) * 512], op=ALU.add))
                for tl in range(4):
                    r0 = (4 * j + tl) * 128
                    kb.dma(A["y"][r0:r0 + 128, :], xo[tl][:], [B_xo[tl]], [], B_xo[tl])
            kb.barrier()
    return nc


def _tile_lists(T, half):
    nown = T // 2
    own = [half * nown + j for j in range(nown)]
    PH = -1
    hb = own[0] - 1 if half == 1 else PH
    ha = own[-1] + 1 if half == 0 else PH
    ql = [hb] + own + [ha]
    rest = [t for t in range(T) if t not in ql]
    return ql + rest, nown


def _rope_table(L):
    rows = L // GRID_W
    row = np.repeat(np.arange(rows, dtype=np.float32), GRID_W)
    col = np.tile(np.arange(GRID_W, dtype=np.float32), rows)
    inv = (np.float32(10000.0) ** (-np.arange(0, 64, 2, dtype=np.float32) / np.float32(64))).astype(np.float32)
    ar = (row[:, None] * inv[None, :]).astype(np.float32)
    ac = (col[:, None] * inv[None, :]).astype(np.float32)
    return np.concatenate([np.cos(ar), np.cos(ac), np.sin(ar), np.sin(ac)], axis=1).astype(np.float32)


def _bmats(half):
    out = np.zeros((20, 128, 128), np.float32)
    s = np.arange(128)[:, None]
    d = np.arange(128)[None, :]
    for g, w in enumerate(POOL_WINDOWS):
        h = w // 2
        inwin = lambda srcpos: ((srcpos >= d - h) & (srcpos <= d + h - 1)).astype(np.float32)
        out[g] = inwin(s - 128) / w
        out[4 + g] = inwin(s + 128) / w
        eye = (s == d).astype(np.float32)
        cint = inwin(s) / w - eye
        cnt_f = ((d + h) - np.maximum(d - h, 0)).astype(np.float32)
        cfirst = inwin(s) / cnt_f - eye
        cnt_l = (np.minimum(d + h, 128) - (d - h)).astype(np.float32)
        clast = inwin(s) / cnt_l - eye
        out[8 + g] = cfirst if half == 0 else cint
        out[12 + g] = cint
        out[16 + g] = clast if half == 1 else cint
    return np.ascontiguousarray(out.transpose(1, 0, 2).reshape(128, 20 * 128))


def _core_inputs(c, xs_seq, xp_seq, common, Ts, Tp, rope_s, rope_p):
    half = c % 2
    m = dict(common)
    for name, xseq, T, rope in (("s", xs_seq, Ts, rope_s), ("p", xp_seq, Tp, rope_p)):
        kl, nown = _tile_lists(T, half)
        NT = T + 1
        x = np.zeros((NT * 128, D), np.float32)
        rp = np.zeros((NT * 128, 128), np.float32)
        kbm = np.zeros((128, NT), np.float32)
        for i, t in enumerate(kl):
            if t < 0:
                kbm[:, i] = NEG
                rp[i * 128:(i + 1) * 128, 0:64] = 1.0
            else:
                x[i * 128:(i + 1) * 128] = xseq[t * 128:(t + 1) * 128]
                rp[i * 128:(i + 1) * 128] = rope[t * 128:(t + 1) * 128]
        m["x_" + name] = x
        m["rope_" + name] = rp
        m["kb_" + name] = kbm
    m["cmask"] = np.tile(np.array([[1.0 if half == 1 else 0.0, 1.0 if half == 0 else 0.0]], np.float32), (128, 1))
    m["bmat"] = _bmats(half)
    return m


def _common(g_norm_mix, w_in, g_q, g_k, w_pool, pool_scale, w_out, g_norm_ffn, w_up, w_conv, b_conv, w_down):
    f = lambda a: np.ascontiguousarray(np.asarray(a, np.float32))
    return dict(
        w_in=f(w_in[0]), w_out=f(w_out[0]), w_up=f(w_up[0]), w_down=f(w_down[0]), w_pool=f(np.asarray(w_pool[0]).reshape(1024, 256)),
        gmix=f(np.asarray(g_norm_mix[0]).reshape(NKC, 128).T), gffn=f(np.asarray(g_norm_ffn[0]).reshape(NKC, 128).T),
        gq=f(np.tile(np.asarray(g_q[0])[None, :], (128, 1))), gk=f(np.tile(np.asarray(g_k[0])[None, :], (128, 1))),
        pscale=f(np.asarray(pool_scale[0]).reshape(8, 128).T),
        wconv=f(np.asarray(w_conv[0]).reshape(3, NFC, 128).transpose(2, 1, 0).reshape(128, NFC * 3)),
        bconv=f(np.asarray(b_conv[0]).reshape(NFC, 128).T),
    )


def run(x_prompt, x_sample, weights, ncores=8, trace=False, debug=False):
    x_prompt = np.asarray(x_prompt, np.float32); x_sample = np.asarray(x_sample, np.float32)
    Bp, Lp, _ = x_prompt.shape
    Bs, Ls, _ = x_sample.shape
    Tp, Ts = Lp // 128, Ls // 128
    cfg = dict(phases=[dict(name="s", T=Ts, nown=Ts // 2), dict(name="p", T=Tp, nown=Tp // 2)], debug=debug)
    nc = build(cfg)
    common = _common(**weights)
    rope_s, rope_p = _rope_table(Ls), _rope_table(Lp)
    in_maps = [_core_inputs(c, x_sample[c // 2], x_prompt[c // 2], common, Ts, Tp, rope_s, rope_p) for c in range(ncores)]
    res = run_bass_kernel_spmd(nc, in_maps, core_ids=list(range(ncores)), **({"trace": True} if trace else {}))
    yp = np.zeros((Bp, Lp, D), np.float32); ys = np.zeros((Bs, Ls, D), np.float32)
    for c in range(ncores):
        s, half = c // 2, c % 2
        r = res.results[c]
        ys[s, half * (Ls // 2):(half + 1) * (Ls // 2)] = r["y_s"]
        yp[s, half * (Lp // 2):(half + 1) * (Lp // 2)] = r["y_p"]
    return (yp, ys), res


def kernel(x_prompt, x_sample, g_norm_mix, w_in, g_q, g_k, w_pool, pool_scale, w_out,
           g_norm_ffn, w_up, w_conv, b_conv, w_down):
    weights = dict(g_norm_mix=g_norm_mix, w_in=w_in, g_q=g_q, g_k=g_k, w_pool=w_pool, pool_scale=pool_scale, w_out=w_out,
                   g_norm_ffn=g_norm_ffn, w_up=w_up, w_conv=w_conv, b_conv=b_conv, w_down=w_down)
    (yp, ys), _ = run(x_prompt, x_sample, weights)
    return (yp, ys)
```

```python
import numpy as np
from contextlib import ExitStack
import concourse.bass as bass
import concourse.mybir as mybir
from concourse.bass_utils import run_bass_kernel_spmd

F32 = mybir.dt.float32
BF16 = mybir.dt.bfloat16
AF = mybir.ActivationFunctionType
ALU = mybir.AluOpType
AX = mybir.AxisListType

D = 2048
NKC = 16
HD = 128
NQH = 8
NKV = 2
INW = 2560
DFF = 5632
NFC = 44
GRID_W = 64
EPS = 1e-6
SCALE = HD ** -0.5
POOL_WINDOWS = (2, 4, 8, 16)
NEG = -30000.0

OPT_E1024 = False
OPT_ACC1024 = True


class Eng:
    def __init__(self, name, h, sem):
        self.name, self.h, self.sem, self.count, self.seen = name, h, sem, 0, {}


class Buf:
    __slots__ = ("name", "w", "r", "dsem", "dcount", "dkey")

    def __init__(self, name):
        self.name = name
        self.w = {}
        self.r = {}
        self.dsem = None
        self.dcount = 0
        self.dkey = None


class KB:
    def __init__(self, nc, es):
        self.nc = nc
        self.es = es
        mk = lambda n: es.enter_context(nc.semaphore(n))
        self.pe = Eng("pe", nc.tensor, mk("s_pe"))
        self.act = Eng("act", nc.scalar, mk("s_act"))
        self.dve = Eng("dve", nc.vector, mk("s_dve"))
        self.pool = Eng("pool", nc.gpsimd, mk("s_pool"))
        self.sp = Eng("sp", nc.sync, None)
        self.engs = [self.pe, self.act, self.dve, self.pool, self.sp]
        self.free_dsems = []
        self.nd = 0
        self.dma_events = {}

    def _waits(self, eng, deps):
        for key, (sem, val) in deps.items():
            if eng is self.pe and key == "pe":
                continue
            if eng.seen.get(key, 0) < val:
                eng.h.wait_ge(sem, val)
                eng.seen[key] = val

    @staticmethod
    def _merge(deps, d):
        for k, (s, v) in d.items():
            if k not in deps or deps[k][1] < v:
                deps[k] = (s, v)

    def _deps(self, reads, writes):
        deps = {}
        for b in reads:
            self._merge(deps, b.w)
        for b in writes:
            self._merge(deps, b.w)
            self._merge(deps, b.r)
        return deps

    def _record(self, key, ev, reads, writes):
        for b in reads:
            if key not in b.r or b.r[key][1] < ev[1]:
                b.r[key] = ev
        for b in writes:
            b.w = {key: ev}
            b.r = {}

    def op(self, eng, reads, writes, fn):
        self._waits(eng, self._deps(reads, writes))
        ins = fn()
        eng.count += 1
        ins.then_inc(eng.sem, 1)
        self._record(eng.name, (eng.sem, eng.count), reads, writes)

    def _dsem(self, buf):
        if buf.dsem is None:
            if self.free_dsems:
                buf.dsem, buf.dcount, buf.dkey = self.free_dsems.pop()
            else:
                self.nd += 1
                buf.dsem = self.es.enter_context(self.nc.semaphore("s_d%d" % self.nd))
                buf.dcount = 0
                buf.dkey = "d%d" % self.nd

    def release(self, bufs):
        for b in bufs:
            if b.dsem is not None:
                self.free_dsems.append((b.dsem, b.dcount, b.dkey))
                b.dsem = None

    def dma(self, out, in_, reads, writes, sbuf, q=None, extra=None):
        q = q or self.sp
        self._dsem(sbuf)
        deps = self._deps(reads, writes)
        if sbuf.dcount > 0:
            self._merge(deps, {sbuf.dkey: (sbuf.dsem, sbuf.dcount)})
        if extra:
            self._merge(deps, extra)
        self._waits(q, deps)
        ins = q.h.dma_start(out=out, in_=in_)
        sbuf.dcount += 16
        ins.then_inc(sbuf.dsem, 16)
        ev = (sbuf.dsem, sbuf.dcount)
        self.dma_events[sbuf.dkey] = ev
        self._record(sbuf.dkey, ev, reads, writes)

    def barrier(self):
        evs = dict(self.dma_events)
        for e in self.engs:
            if e.sem is not None and e.count > 0:
                evs[e.name] = (e.sem, e.count)
        for e in self.engs:
            for key, (sem, val) in evs.items():
                if key == e.name:
                    continue
                if e.seen.get(key, 0) < val:
                    e.h.wait_ge(sem, val)
                    e.seen[key] = val


def V(t, off, dims):
    a = t[:]
    return bass.AP(a.tensor, off, [[a.ap[0][0], 128]] + [[s, c] for s, c in dims])


def build(cfg):
    phases = cfg["phases"]
    nc = bass.Bass("TRN2", target_bir_lowering=False)
    din = lambda n, shp, dt=F32: nc.dram_tensor(n, list(shp), dt, kind="ExternalInput").ap()
    dout = lambda n, shp: nc.dram_tensor(n, list(shp), F32, kind="ExternalOutput").ap()
    dbg = cfg.get("debug", False)
    dint = lambda n, shp, dt=BF16: nc.dram_tensor(n, list(shp), dt, kind=("ExternalOutput" if (dbg and n[:2] in ("QT", "U_", "H2")) else "Internal")).ap()

    P = {}
    dbg = cfg.get("debug", False)
    for ph in phases:
        n, T, nown = ph["name"], ph["T"], ph["nown"]
        NT, NQ = T + 1, nown + 2
        ph["NT"], ph["NQ"] = NT, NQ
        P[n] = dict(
            x=din("x_" + n, [NT * 128, D]),
            rope=din("rope_" + n, [NT * 128, 128]),
            kb=din("kb_" + n, [128, NT]),
            y=dout("y_" + n, [nown * 128, D]),
            QT=dint("QT_" + n, [NQ, 128, NQH * 128]),
            U=dint("U_" + n, [NQ, 128, 1024]),
            H2T=dint("H2T_" + n, [NKC, 128, NQ * 128]),
        )
        if dbg:
            P[n]["dKT"] = nc.dram_tensor("dKT_" + n, [128, NKV * NT * 128], BF16, kind="ExternalOutput").ap()
            P[n]["dV"] = nc.dram_tensor("dV_" + n, [128, NT * 256], BF16, kind="ExternalOutput").ap()
            P[n]["dmix"] = nc.dram_tensor("dmix_" + n, [NQ // 2, 128, NKC * 256], BF16, kind="ExternalOutput").ap()
    w_in = din("w_in", [D, INW]); w_out = din("w_out", [D, D]); w_up = din("w_up", [D, 2 * DFF])
    w_down = din("w_down", [DFF, D]); w_pool = din("w_pool", [1024, 256])
    gmix_d = din("gmix", [128, NKC]); gffn_d = din("gffn", [128, NKC])
    gq_d = din("gq", [128, 128]); gk_d = din("gk", [128, 128])
    pscale_d = din("pscale", [128, 8]); wconv_d = din("wconv", [128, NFC * 3]); bconv_d = din("bconv", [128, NFC])
    cmask_d = din("cmask", [128, 2]); bmat_d = din("bmat", [128, 20 * 128])
    b_in = dint("b_in", [D, INW]); b_out = dint("b_out", [D, D]); b_up = dint("b_up", [D, 2 * DFF])
    b_down = dint("b_down", [DFF, D]); b_pool = dint("b_pool", [1024, 256])

    with ExitStack() as es:
        kb = KB(nc, es)
        pe, act, dve, pool, sp = kb.pe, kb.act, kb.dve, kb.pool, kb.sp

        uid = [0]

        def sbt(st, name, shape, dt):
            uid[0] += 1
            return st.enter_context(nc.sbuf_tensor("sb%d_%s" % (uid[0], name), list(shape), dt))

        ps = es.enter_context(nc.psum_tensor("ps", [128, 8, 512], F32))
        bankB = [Buf("bank%d" % i) for i in range(8)]
        bank = lambda i: ps[:, i, :]
        psbf = lambda i, n=1: V(ps, i * 512, [(1, 512 * n)]).bitcast(BF16)

        ident = sbt(es, "ident", [128, 128], BF16); identf = sbt(es, "identf", [128, 128], F32)
        ones = sbt(es, "ones", [128, 128], BF16); onesf = sbt(es, "onesf", [128, 128], F32)
        gmix = sbt(es, "gmix", [128, NKC], F32); gffn = sbt(es, "gffn", [128, NKC], F32)
        gq = sbt(es, "gq", [128, 128], F32); gk = sbt(es, "gk", [128, 128], F32)
        pscale = sbt(es, "pscale", [128, 8], F32); wconv = sbt(es, "wconv", [128, NFC * 3], F32)
        bconv = sbt(es, "bconv", [128, NFC], F32); cmask = sbt(es, "cmask", [128, 2], F32)
        epsT = sbt(es, "epsT", [128, 1], F32); negc = sbt(es, "negc", [128, 1], F32)
        cst = sbt(es, "cst", [128, 4], F32)
        B_c = Buf("consts")
        for i, (t, d_) in enumerate([(gmix, gmix_d), (gffn, gffn_d), (gq, gq_d), (gk, gk_d), (pscale, pscale_d),
                                     (wconv, wconv_d), (bconv, bconv_d), (cmask, cmask_d)]):
            kb.dma(t[:], d_, [], [B_c], B_c)
        B_id = Buf("ident")
        kb.op(pool, [], [B_id], lambda: nc.gpsimd.memset(identf[:], 0.0))
        kb.op(pool, [], [B_id], lambda: nc.gpsimd.affine_select(out=identf[:], in_=identf[:], compare_op=ALU.not_equal,
                                                                 fill=1.0, base=0, pattern=[[-1, 128]], channel_multiplier=1))
        B_id2 = Buf("ident2")
        kb.op(dve, [B_id], [B_id2], lambda: nc.vector.tensor_copy(ident[:], identf[:]))
        kb.op(dve, [], [B_id2], lambda: nc.vector.memset(ones[:], 1.0))
        kb.op(dve, [], [B_id2], lambda: nc.vector.memset(onesf[:], 1.0))
        kb.op(dve, [], [B_id2], lambda: nc.vector.memset(epsT[:], EPS))
        gsq = sbt(es, "gsq", [128, 256], F32)
        kb.op(dve, [B_c], [B_id2], lambda: nc.vector.tensor_tensor(out=gsq[:, 0:128], in0=gq[:], in1=gq[:], op=ALU.mult))
        kb.op(dve, [B_c, B_id2], [B_id2], lambda: nc.vector.tensor_tensor(out=gsq[:, 128:256], in0=gk[:], in1=gk[:], op=ALU.mult))
        kb.op(dve, [B_id2], [B_id2], lambda: nc.vector.tensor_reduce(out=cst[:, 0:2], in_=gsq[:].rearrange("p (a b) -> p a b", b=128), axis=AX.X, op=ALU.max))
        kb.op(dve, [B_id2], [B_id2], lambda: nc.vector.tensor_tensor(out=cst[:, 2:3], in0=cst[:, 0:1], in1=cst[:, 1:2], op=ALU.mult))
        kb.op(act, [B_id2], [B_id2], lambda: nc.scalar.activation(out=cst[:, 3:4], in_=cst[:, 2:3], func=AF.Ln, scale=float(HD)))
        kb.op(act, [B_id2], [B_id2], lambda: nc.scalar.activation(out=cst[:, 2:3], in_=cst[:, 3:4], func=AF.Exp, scale=0.5))
        kb.op(dve, [B_id2], [B_id2], lambda: nc.vector.tensor_scalar(out=negc[:], in0=cst[:, 2:3], scalar1=-1.0, scalar2=None, op0=ALU.mult))
        CONST = [B_c, B_id2]

        conv_ev = {}

        def convert(name, src, dst, rows, rchunk):
            b = Buf("cv_" + name)
            kb._dsem(b)
            n = 0
            for r0 in range(0, rows, rchunk):
                r1 = min(rows, r0 + rchunk)
                nc.gpsimd.dma_start(out=dst[r0:r1, :], in_=src[r0:r1, :]).then_inc(b.dsem, 16)
                n += 1
            b.dcount += 16 * n
            conv_ev[name] = {b.dkey: (b.dsem, b.dcount)}
            kb.dma_events[b.dkey] = (b.dsem, b.dcount)

        def emit_conversions():
            convert("pool", w_pool, b_pool, 1024, 128)
            convert("out", w_out, b_out, D, 128)

        conv_tasks = []

        def plan_convert(name, src, dst, rows, rchunk):
            b = Buf("cv_" + name)
            kb._dsem(b)
            n = 0
            for r0 in range(0, rows, rchunk):
                r1 = min(rows, r0 + rchunk)
                conv_tasks.append((b, dst[r0:r1, :], src[r0:r1, :]))
                n += 1
            conv_ev[name] = {b.dkey: (b.dsem, b.dcount + 16 * n)}

        def run_conv_tasks(k):
            if not conv_tasks:
                return
            if pe.count > 0 and pool.seen.get("pe", 0) < pe.count:
                nc.gpsimd.wait_ge(pe.sem, pe.count)
                pool.seen["pe"] = pe.count
            for _ in range(min(k, len(conv_tasks))):
                b, dst, src = conv_tasks.pop(0)
                nc.gpsimd.dma_start(out=dst, in_=src).then_inc(b.dsem, 16)
                b.dcount += 16
                kb.dma_events[b.dkey] = (b.dsem, b.dcount)

        def emit_conversions_late():
            plan_convert("in", w_in, b_in, D, 128)
            plan_convert("up", w_up, b_up, D, 128)
            plan_convert("down", w_down, b_down, DFF, 128)

        def rms_rstd(eng_sq, src_ap, junk_ap, ss, rstd, n, rd, wr):
            kb.op(dve, [], wr, lambda: nc.vector.memset(ss, 0.0))
            kb.op(act, rd + wr, wr, lambda: nc.scalar.activation(out=junk_ap, in_=src_ap, func=AF.Square, accum_out=ss))
            kb.op(act, wr + CONST, wr, lambda: nc.scalar.activation(out=ss, in_=ss, func=AF.Ln, bias=epsT[:], scale=1.0 / n))
            kb.op(act, wr, wr, lambda: nc.scalar.activation(out=rstd, in_=ss, func=AF.Exp, scale=-0.5))

        for ph in phases:
            n, T, nown, NT, NQ = ph["name"], ph["T"], ph["nown"], ph["NT"], ph["NQ"]
            A = P[n]
            with ExitStack() as pst:
                KT = sbt(pst, "KT_" + n, [128, NKV, NT * 128], BF16)
                VV = sbt(pst, "V_" + n, [128, NT, 256], BF16)
                B_KT = [Buf("KT%d" % i) for i in range(NT)]
                B_V = [Buf("V%d" % i) for i in range(NT)]
                with ExitStack() as st:
                    win = sbt(st, "win", [128, NKC, INW], BF16)
                    xt = sbt(st, "xt", [128, D], F32)
                    xh = sbt(st, "xh", [128, D], BF16)
                    hT = [sbt(st, "hT%d" % i, [128, NKC, 128], BF16) for i in range(2)]
                    z = [sbt(st, "z%d" % i, [128, 1280], F32) for i in range(2)]
                    qn = [sbt(st, "qn%d" % i, [128, 1280], F32) for i in range(2)]
                    t1 = sbt(st, "t1", [128, 640], F32); t2 = sbt(st, "t2", [128, 640], F32)
                    qkb = sbt(st, "qkb", [128, 1280], BF16)
                    ub = [sbt(st, "ub%d" % i, [128, 1024], BF16) for i in range(2)]
                    qTs = [sbt(st, "qTs%d" % i, [128, NQH * 128], BF16) for i in range(2)]
                    rt = [sbt(st, "rt%d" % i, [128, 128], F32) for i in range(2)]
                    stF = sbt(st, "stF", [128, 4], F32)
                    stB = [sbt(st, "stB%d" % i, [128, 32], F32) for i in range(2)]
                    junk = sbt(st, "junk", [128, 128], BF16)
                    B_win = Buf("win"); B_xt = Buf("xt"); B_xh = Buf("xh"); B_hT = [Buf("hT0"), Buf("hT1")]
                    B_z = [Buf("z0"), Buf("z1")]; B_qn = [Buf("qn0"), Buf("qn1")]; B_t1 = Buf("t1"); B_t2 = Buf("t2")
                    B_qkb = Buf("qkb"); B_ub = [Buf("ub0"), Buf("ub1")]; B_qTs = [Buf("qTs0"), Buf("qTs1")]
                    B_rt = [Buf("rt0"), Buf("rt1")]; B_stF = Buf("stF"); B_stB = [Buf("stB0"), Buf("stB1")]; B_junk = Buf("junk")
                    p1bufs = [B_win, B_xt, B_ub[0], B_ub[1], B_qTs[0], B_qTs[1], B_rt[0], B_rt[1]]
                    CB = [1024, 0, 512, 1536, 2048]
                    B_wc = {c0: Buf("win_c%d" % c0) for c0 in CB}
                    if ph is not phases[0]:
                        p1bufs = p1bufs + list(B_wc.values())
                    if ph is phases[0]:
                        fsrc = w_in.rearrange("(kc p) n -> p kc n", p=128)
                        for c0 in CB:
                            kb.dma(win[:, :, c0:c0 + 512], fsrc[:, :, c0:c0 + 512], [], [B_wc[c0]], B_wc[c0], q=pool)
                        emit_conversions()
                    else:
                        bsrc = b_in.rearrange("(kc p) n -> p kc n", p=128)
                        for c0 in CB:
                            kb.dma(win[:, :, c0:c0 + 512], bsrc[:, :, c0:c0 + 512], [], [B_wc[c0]], B_wc[c0], extra=conv_ev["in"])

                    def load_x(i):
                        kb.dma(xt[:], A["x"][i * 128:(i + 1) * 128, :], [], [B_xt], B_xt)

                    def front_a(i, hb, inext):
                        isq = i < NQ
                        kb.dma(rt[hb][:], A["rope"][i * 128:(i + 1) * 128, :], [], [B_rt[hb]], B_rt[hb])
                        rms_rstd(act, xt[:], xh[:], stF[:, 0:1], stF[:, 1:2], D, [B_xt], [B_stF, B_xh])
                        kb.op(dve, [B_xt, B_stF], [B_xh], lambda: nc.vector.tensor_scalar(out=xh[:], in0=xt[:], scalar1=stF[:, 1:2], scalar2=None, op0=ALU.mult))
                        if inext is not None:
                            load_x(inext)

                    def front_a2(i, hb):
                        isq = i < NQ

                        def tr16():
                            o = psbf(0, 2)
                            for kc in range(NKC):
                                ins = nc.tensor.transpose(out=o[:, kc * 128:(kc + 1) * 128], in_=xh[:, kc * 128:(kc + 1) * 128], identity=ident[:])
                            return ins
                        kb.op(pe, [B_xh] + CONST, [bankB[0], bankB[1]], tr16)
                        kb.op(dve, [bankB[0], bankB[1]] + CONST, [B_hT[hb]], lambda: nc.vector.tensor_tensor(
                            out=hT[hb][:], in0=psbf(0, 2).rearrange("p (a b) -> p a b", b=128),
                            in1=gmix[:].unsqueeze(2).broadcast_to([128, NKC, 128]), op=ALU.mult))

                        def mm_in(b_, c0):
                            def f():
                                for kc in range(NKC):
                                    ins = nc.tensor.matmul(bank(b_), lhsT=hT[hb][:, kc, :], rhs=win[:, kc, c0:c0 + 512], start=(kc == 0), stop=(kc == NKC - 1))
                                return ins
                            return f
                        kb.op(pe, [B_hT[hb], B_wc[1024]], [bankB[2]], mm_in(2, 1024))
                        if isq:
                            kb.op(pe, [B_hT[hb], B_wc[0]], [bankB[3]], mm_in(3, 0))
                            kb.op(pe, [B_hT[hb], B_wc[512]], [bankB[4]], mm_in(4, 512))
                            kb.op(pe, [B_hT[hb], B_wc[1536]], [bankB[5]], mm_in(5, 1536))
                            kb.op(pe, [B_hT[hb], B_wc[2048]], [bankB[6]], mm_in(6, 2048))

                    def front_b(i, hb):
                        isq = i < NQ
                        kb.op(act, [bankB[2]], [B_V[i]], lambda: nc.scalar.copy(out=VV[:, i, :], in_=ps[:, 2, 256:512]))
                        kb.op(act, [bankB[2]], [B_z[hb]], lambda: nc.scalar.copy(out=z[hb][:, 1024:1280], in_=ps[:, 2, 0:256]))
                        if isq:
                            kb.op(act, [bankB[3], bankB[4]], [B_z[hb]], lambda: nc.scalar.copy(out=z[hb][:, 0:1024], in_=V(ps, 3 * 512, [(1, 1024)])))
                            kb.op(act, [bankB[5], bankB[6]], [B_ub[hb]], lambda: nc.scalar.copy(out=ub[hb][:], in_=V(ps, 5 * 512, [(1, 1024)])))
                            kb.dma(A["U"][i], ub[hb][:], [B_ub[hb]], [], B_ub[hb])

                    def back(i, hb):
                        isq = i < NQ
                        h0 = 0 if isq else 8
                        H = 10 - h0
                        zz, qq, sB = z[hb], qn[hb], stB[hb]
                        zs = zz[:, h0 * 128:1280]; qs = qq[:, h0 * 128:1280]
                        kb.op(dve, [], [B_stB[hb]], lambda: nc.vector.memset(sB[:, 0:16], 0.0))
                        for h in range(h0, 10):
                            kb.op(act, [B_z[hb], B_stB[hb]], [B_stB[hb], B_junk], lambda h=h: nc.scalar.activation(
                                out=junk[:], in_=zz[:, h * 128:(h + 1) * 128], func=AF.Square, accum_out=sB[:, h:h + 1]))
                        kb.op(act, [B_stB[hb]] + CONST, [B_stB[hb]], lambda: nc.scalar.activation(out=sB[:, h0:10], in_=sB[:, h0:10], func=AF.Ln, bias=epsT[:], scale=1.0 / HD))
                        kb.op(act, [B_stB[hb]], [B_stB[hb]], lambda: nc.scalar.activation(out=sB[:, 16 + h0:26], in_=sB[:, h0:10], func=AF.Exp, scale=-0.5))
                        kb.op(dve, [B_z[hb], B_stB[hb]], [B_qn[hb]], lambda: nc.vector.tensor_tensor(
                            out=qs.rearrange("p (h d) -> p h d", d=128), in0=zs.rearrange("p (h d) -> p h d", d=128),
                            in1=sB[:, 16 + h0:26].unsqueeze(2).broadcast_to([128, H, 128]), op=ALU.mult))
                        if isq:
                            kb.op(dve, [B_qn[hb]] + CONST, [B_qn[hb]], lambda: nc.vector.tensor_tensor(
                                out=qq[:, 0:1024].rearrange("p (h d) -> p h d", d=128), in0=qq[:, 0:1024].rearrange("p (h d) -> p h d", d=128),
                                in1=gq[:].unsqueeze(1).broadcast_to([128, 8, 128]), op=ALU.mult))
                        kb.op(dve, [B_qn[hb]] + CONST, [B_qn[hb]], lambda: nc.vector.tensor_tensor(
                            out=qq[:, 1024:1280].rearrange("p (h d) -> p h d", d=128), in0=qq[:, 1024:1280].rearrange("p (h d) -> p h d", d=128),
                            in1=gk[:].unsqueeze(1).broadcast_to([128, 2, 128]), op=ALU.mult))
                        x1 = V(qq, h0 * 128, [(128, H), (64, 2), (1, 32)]); x2 = V(qq, h0 * 128 + 32, [(128, H), (64, 2), (1, 32)])
                        o1 = V(qkb, h0 * 128, [(128, H), (64, 2), (1, 32)]); o2 = V(qkb, h0 * 128 + 32, [(128, H), (64, 2), (1, 32)])
                        cc = V(rt[hb], 0, [(0, H), (32, 2), (1, 32)]); ss_ = V(rt[hb], 64, [(0, H), (32, 2), (1, 32)])
                        ta = V(t1, 0, [(64, H), (32, 2), (1, 32)]); tb = V(t2, 0, [(64, H), (32, 2), (1, 32)])
                        kb.op(dve, [B_qn[hb], B_rt[hb]], [B_t1], lambda: nc.vector.tensor_tensor(out=ta, in0=x1, in1=cc, op=ALU.mult))
                        kb.op(dve, [B_qn[hb], B_rt[hb]], [B_t2], lambda: nc.vector.tensor_tensor(out=tb, in0=x2, in1=ss_, op=ALU.mult))
                        kb.op(dve, [B_t1, B_t2], [B_qkb], lambda: nc.vector.tensor_tensor(out=o1, in0=ta, in1=tb, op=ALU.subtract))
                        kb.op(dve, [B_qn[hb], B_rt[hb], B_qkb], [B_t1], lambda: nc.vector.tensor_tensor(out=ta, in0=x2, in1=cc, op=ALU.mult))
                        kb.op(dve, [B_qn[hb], B_rt[hb], B_qkb], [B_t2], lambda: nc.vector.tensor_tensor(out=tb, in0=x1, in1=ss_, op=ALU.mult))
                        kb.op(dve, [B_t1, B_t2, B_qkb], [B_qkb], lambda: nc.vector.tensor_tensor(out=o2, in0=ta, in1=tb, op=ALU.add))

                        def trk():
                            o = psbf(7)
                            for hk in range(2):
                                ins = nc.tensor.transpose(out=o[:, hk * 128:(hk + 1) * 128], in_=qkb[:, (8 + hk) * 128:(9 + hk) * 128], identity=ident[:])
                            return ins
                        kb.op(pe, [B_qkb] + CONST, [bankB[7]], trk)
                        kb.op(act, [bankB[7]], [B_KT[i]], lambda: nc.scalar.copy(
                            out=V(KT, i * 128, [(NT * 128, 2), (1, 128)]), in_=psbf(7)[:, 0:256].rearrange("p (h t) -> p h t", t=128)))
                        if isq:
                            def trq():
                                o = psbf(7)
                                for hq in range(8):
                                    ins = nc.tensor.transpose(out=o[:, hq * 128:(hq + 1) * 128], in_=qkb[:, hq * 128:(hq + 1) * 128], identity=ident[:])
                                return ins
                            kb.op(pe, [B_qkb] + CONST, [bankB[7]], trq)
                            kb.op(dve, [bankB[7]], [B_qTs[hb]], lambda: nc.vector.tensor_copy(qTs[hb][:], psbf(7)))
                            kb.dma(A["QT"][i], qTs[hb][:], [B_qTs[hb]], [], B_qTs[hb])

                    order = list(range(NQ, NT)) + list(range(NQ))
                    nxt = lambda p: order[p + 1] if p + 1 < NT else None
                    load_x(order[0])
                    front_a(order[0], 0, nxt(0))
                    front_a2(order[0], 0)
                    for p in range(NT):
                        if p + 1 < NT:
                            front_a(order[p + 1], (p + 1) % 2, nxt(p + 1))
                        front_b(order[p], p % 2)
                        if p + 1 < NT:
                            front_a2(order[p + 1], (p + 1) % 2)
                        back(order[p], p % 2)
                    if dbg:
                        B_dbg = Buf("dbg")
                        kb.dma(A["dKT"], KT[:].rearrange("p a b -> p (a b)"), B_KT, [], B_dbg)
                        kb.dma(A["dV"], VV[:].rearrange("p a b -> p (a b)"), B_V, [], B_dbg)
                    kb.barrier()
                    kb.release(p1bufs)
                    if ph is phases[0]:
                        emit_conversions_late()

                with ExitStack() as st:
                    wo = [sbt(st, "wo%d" % i, [128, NKC, 512], BF16) for i in range(2)]
                    QTg = [sbt(st, "QTg%d" % i, [128, NQH, 256], BF16) for i in range(2)]
                    ug = [sbt(st, "ug%d" % i, [128, 4, 1024], BF16) for i in range(2)]
                    PT = [sbt(st, "PT%d" % i, [128, 1024], BF16) for i in range(4)]
                    rl = sbt(st, "rl", [128, 1024], F32); Lacc = sbt(st, "Lacc", [128, 1024], F32)
                    B_acc = [Buf("acc0"), Buf("acc1")]
                    mixT = sbt(st, "mixT", [128, NKC, 256], BF16)
                    dT = sbt(st, "dT", [128, 8, 256], BF16)
                    xm = [sbt(st, "xm%d" % i, [128, D], F32) for i in range(2)]
                    sqj = sbt(st, "sqj2", [128, D], BF16); h2 = sbt(st, "h2", [128, D], BF16)
                    h2T = [sbt(st, "h2T%d" % i, [128, NKC, 128], BF16) for i in range(2)]
                    kbm = sbt(st, "kbm", [128, NT], F32); kbs = sbt(st, "kbs", [128, NT], F32)
                    bmf = sbt(st, "bmf", [128, 20 * 128], F32); bmat = sbt(st, "bmat", [128, 20, 128], BF16)
                    wpool = sbt(st, "wpool", [128, 8, 256], BF16)
                    stt = sbt(st, "stt2", [128, 8], F32)
                    B_wo = [Buf("wo0"), Buf("wo1")]; B_QTg = [Buf("QTg0"), Buf("QTg1")]; B_ug = [Buf("ug0"), Buf("ug1")]
                    B_PT = [[Buf("PT%d_%d" % (i, j)) for j in range(2)] for i in range(4)]; B_rl = Buf("rl"); B_mixT = Buf("mixT"); B_dT = Buf("dT")
                    B_xm = [Buf("xm0"), Buf("xm1")]; B_sqj = Buf("sqj"); B_h2 = Buf("h2"); B_h2T = [Buf("h2T0"), Buf("h2T1")]
                    B_kb = Buf("kb"); B_bm = Buf("bm"); B_wp = Buf("wpool"); B_st = Buf("st")
                    p2bufs = B_wo + B_QTg + B_ug + B_xm + B_h2T + [B_kb, B_bm, B_wp]
                    kb.dma(kbm[:], A["kb"], [], [B_kb], B_kb)
                    kb.op(dve, [B_kb] + CONST, [B_kb], lambda: nc.vector.tensor_scalar(out=kbs[:], in0=kbm[:], scalar1=negc[:], scalar2=None, op0=ALU.add))
                    kb.dma(bmf[:], bmat_d, [], [B_bm], B_bm)
                    kb.op(dve, [B_bm], [B_bm], lambda: nc.vector.tensor_copy(bmat[:].rearrange("p a b -> p (a b)"), bmf[:]))
                    kb.dma(wpool[:], b_pool.rearrange("(a p) n -> p a n", p=128), [], [B_wp], B_wp, extra=conv_ev["pool"])
                    NG = NQ // 2
                    wcnt = [0]

                    def load_group(g):
                        gb = g % 2
                        for tl in range(2):
                            kb.dma(QTg[gb][:, :, tl * 128:(tl + 1) * 128], A["QT"][2 * g + tl].rearrange("p (h t) -> p h t", t=128), [], [B_QTg[gb]], B_QTg[gb])
                        lo, hi = max(2 * g - 1, 0), min(2 * g + 2, NQ - 1)
                        kb.dma(ug[gb][:, 0:hi - lo + 1, :], A["U"][lo:hi + 1].rearrange("a p n -> p a n"), [], [B_ug[gb]], B_ug[gb])
                    load_group(0)
                    for g in range(NG):
                        gb = g % 2
                        for tl in range(2):
                            ti = 2 * g + tl
                            kb.dma(xm[tl][:], A["x"][ti * 128:(ti + 1) * 128, :], [], [B_xm[tl]], B_xm[tl])
                        bsrc = b_out.rearrange("(kc p) n -> p kc n", p=128)

                        def wo_dma(nn):
                            kb.dma(wo[nn % 2][:], bsrc[:, :, nn * 512:(nn + 1) * 512], [], [B_wo[nn % 2]], B_wo[nn % 2], extra=conv_ev["out"])
                        wo_dma(0)
                        wo_dma(1)
                        if conv_tasks:
                            run_conv_tasks(-(-76 // max(NG - 1, 1)) if g + 1 < NG else len(conv_tasks))
                        for kvh in range(NKV):
                            SB = [(0, 1), (2, 3), (6, 7)]

                            def s_step(kt):
                                b0, b1 = SB[kt % 3]

                                def f():
                                    for hp, bk in enumerate((b0, b1)):
                                        ins = nc.tensor.matmul(bank(bk), lhsT=KT[:, kvh, kt * 128:(kt + 1) * 128],
                                                               rhs=V(QTg[gb], (kvh * 4 + hp * 2) * 256, [(1, 512)]), start=True, stop=True)
                                    return ins
                                kb.op(pe, [B_KT[kt], B_QTg[gb]], [bankB[b0], bankB[b1]], f)

                            def e_step(kt):
                                r = kt % 4
                                for hp, bk in enumerate(SB[kt % 3]):
                                    kb.op(act, [bankB[bk], B_kb], [B_PT[r][hp]], lambda hp=hp, bk=bk: nc.scalar.activation(
                                        out=PT[r][:, hp * 512:(hp + 1) * 512], in_=bank(bk), func=AF.Exp, bias=kbs[:, kt:kt + 1], scale=SCALE))

                            def pv_step(kt):
                                r = kt % 4

                                def f():
                                    for hp in range(2):
                                        ins = nc.tensor.matmul(bank(4 + hp), lhsT=VV[:, kt, kvh * 128:(kvh + 1) * 128],
                                                               rhs=PT[r][:, hp * 512:(hp + 1) * 512], start=(kt == 0), stop=(kt == NT - 1))
                                    return ins
                                kb.op(pe, B_PT[r] + [B_V[kt]], [bankB[4], bankB[5]], f)
                                if kt == 0:
                                    kb.op(dve, B_PT[r], [B_acc[0]], lambda: nc.vector.tensor_copy(Lacc[:], PT[r][:]))
                                else:
                                    kb.op(dve, B_PT[r] + [B_acc[0]], [B_acc[0]], lambda: nc.vector.tensor_tensor(out=Lacc[:], in0=Lacc[:], in1=PT[r][:], op=ALU.add))
                            s_step(0)
                            if NT > 1:
                                s_step(1)
                            for kt in range(NT):
                                if kt + 2 < NT:
                                    s_step(kt + 2)
                                e_step(kt)
                                pv_step(kt)

                            def l_mm():
                                for hp in range(2):
                                    ins = nc.tensor.matmul(bank(6 + hp), lhsT=onesf[:], rhs=Lacc[:, hp * 512:(hp + 1) * 512], start=True, stop=True)
                                return ins
                            kb.op(pe, B_acc + CONST, [bankB[6], bankB[7]], l_mm)
                            for hp in range(2):
                                kb.op(act, [bankB[6 + hp]], [B_rl], lambda hp=hp: nc.scalar.activation(out=rl[:, hp * 512:(hp + 1) * 512], in_=bank(6 + hp), func=AF.Ln))
                            for hp in range(2):
                                kb.op(act, [B_rl], [B_rl], lambda hp=hp: nc.scalar.activation(out=rl[:, hp * 512:(hp + 1) * 512], in_=rl[:, hp * 512:(hp + 1) * 512], func=AF.Exp, scale=-1.0))
                            for hp in range(2):
                                kb.op(dve, [bankB[4 + hp], B_rl], [B_mixT], lambda hp=hp: nc.vector.tensor_tensor(
                                    out=V(mixT, (kvh * 4 + hp * 2) * 256, [(1, 512)]), in0=bank(4 + hp), in1=rl[:, hp * 512:(hp + 1) * 512], op=ALU.mult))
                        lo = max(2 * g - 1, 0)

                        def pool_mm():
                            for cc in range(8):
                                g4 = cc // 2
                                for tl in range(2):
                                    ti = 2 * g + tl
                                    cur = 8 + g4 if ti == 1 else (16 + g4 if ti == NQ - 2 else 12 + g4)
                                    srcs = [(ti, cur)]
                                    if ti - 1 >= 0:
                                        srcs.append((ti - 1, g4))
                                    if ti + 1 <= NQ - 1:
                                        srcs.append((ti + 1, 4 + g4))
                                    for si, (tsrc, bi) in enumerate(srcs):
                                        ins = nc.tensor.matmul(ps[:, cc // 2, (cc % 2) * 256 + tl * 128:(cc % 2) * 256 + (tl + 1) * 128],
                                                               lhsT=ug[gb][:, tsrc - lo, cc * 128:(cc + 1) * 128], rhs=bmat[:, bi, :],
                                                               start=(si == 0), stop=(si == len(srcs) - 1))
                            return ins
                        kb.op(pe, [B_ug[gb], B_bm], [bankB[0], bankB[1], bankB[2], bankB[3]], pool_mm)
                        for b_ in range(4):
                            e_ = act if b_ % 2 == 0 else dve
                            if e_ is act:
                                kb.op(act, [bankB[b_]], [B_dT], lambda b_=b_: nc.scalar.copy(out=V(dT, b_ * 512, [(1, 512)]), in_=bank(b_)))
                            else:
                                kb.op(dve, [bankB[b_]], [B_dT], lambda b_=b_: nc.vector.tensor_copy(V(dT, b_ * 512, [(1, 512)]), bank(b_)))

                        def pool_lin():
                            for jc in range(8):
                                g4, dc = jc // 2, jc % 2
                                for k2 in range(2):
                                    ins = nc.tensor.matmul(ps[:, 4 + jc // 2, (jc % 2) * 256:(jc % 2 + 1) * 256], lhsT=wpool[:, g4 * 2 + k2, dc * 128:(dc + 1) * 128],
                                                           rhs=dT[:, g4 * 2 + k2, :], start=(k2 == 0), stop=(k2 == 1))
                            return ins
                        kb.op(pe, [B_dT, B_wp], [bankB[4], bankB[5], bankB[6], bankB[7]], pool_lin)
                        B_mx8 = [Buf("mx%d" % jc) for jc in range(8)]
                        for jc in range(8):
                            if True:
                                kb.op(dve, [bankB[4 + jc // 2], B_mixT] + CONST, [B_mx8[jc]], lambda jc=jc: nc.vector.tensor_scalar(
                                    out=mixT[:, 8 + jc, :], in0=ps[:, 4 + jc // 2, (jc % 2) * 256:(jc % 2 + 1) * 256], scalar1=pscale[:, jc:jc + 1], scalar2=None, op0=ALU.mult))
                            else:
                                kb.op(act, [bankB[4 + jc // 2], B_mixT] + CONST, [B_mx8[jc]], lambda jc=jc: nc.scalar.activation(
                                    out=mixT[:, 8 + jc, :], in_=ps[:, 4 + jc // 2, (jc % 2) * 256:(jc % 2 + 1) * 256], func=AF.Copy, scale=pscale[:, jc:jc + 1]))
                        kb.op(dve, B_mx8, [B_mixT], lambda: nc.vector.memset(stt[:, 4:5], 0.0))
                        if dbg:
                            kb.dma(A["dmix"][g], mixT[:].rearrange("p a b -> p (a b)"), [B_mixT], [], B_mixT)
                        if g + 1 < NG:
                            load_group(g + 1)
                        for nn in range(4):
                            wb = nn % 2
                            if nn >= 2:
                                wo_dma(nn)
                            for tl in range(2):
                                ob = (nn * 2 + tl) % 4

                                def mm_o(ob=ob, tl=tl, wb=wb):
                                    for kc in range(NKC):
                                        ins = nc.tensor.matmul(bank(ob), lhsT=mixT[:, kc, tl * 128:(tl + 1) * 128], rhs=wo[wb][:, kc, :], start=(kc == 0), stop=(kc == NKC - 1))
                                    return ins
                                kb.op(pe, [B_mixT, B_wo[wb]], [bankB[ob]], mm_o)
                                kb.op(dve, [bankB[ob], B_xm[tl]], [B_xm[tl]], lambda ob=ob, tl=tl, nn=nn: nc.vector.tensor_tensor(
                                    out=xm[tl][:, nn * 512:(nn + 1) * 512], in0=bank(ob), in1=xm[tl][:, nn * 512:(nn + 1) * 512], op=ALU.add))
                        h2x = [h2, sqj]
                        B_h2x = [B_h2, B_sqj]
                        B_st2 = [Buf("st2_0"), Buf("st2_1")]
                        for tl in range(2):
                            rms_rstd(act, xm[tl][:], h2T[tl][:].rearrange("p a b -> p (a b)"), stt[:, 2 * tl:2 * tl + 1], stt[:, 2 * tl + 1:2 * tl + 2], D,
                                     [B_xm[tl]], [B_st2[tl], B_h2T[tl]])
                            kb.op(dve, [B_xm[tl], B_st2[tl]], [B_h2x[tl]], lambda tl=tl: nc.vector.tensor_scalar(
                                out=h2x[tl][:], in0=xm[tl][:], scalar1=stt[:, 2 * tl + 1:2 * tl + 2], scalar2=None, op0=ALU.mult))
                        for tl in range(2):
                            ti = 2 * g + tl

                            def tr2(tl=tl):
                                o = psbf(4 + 2 * tl, 2)
                                for kc in range(NKC):
                                    ins = nc.tensor.transpose(out=o[:, kc * 128:(kc + 1) * 128], in_=h2x[tl][:, kc * 128:(kc + 1) * 128], identity=ident[:])
                                return ins
                            kb.op(pe, [B_h2x[tl]] + CONST, [bankB[4 + 2 * tl], bankB[5 + 2 * tl]], tr2)
                            kb.op(dve, [bankB[4 + 2 * tl], bankB[5 + 2 * tl]] + CONST, [B_h2T[tl]], lambda tl=tl: nc.vector.tensor_tensor(
                                out=h2T[tl][:], in0=psbf(4 + 2 * tl, 2).rearrange("p (a b) -> p a b", b=128),
                                in1=gffn[:].unsqueeze(2).broadcast_to([128, NKC, 128]), op=ALU.mult))
                            kb.dma(A["H2T"][:, :, ti * 128:(ti + 1) * 128].rearrange("a p t -> p a t"), h2T[tl][:], [B_h2T[tl]], [], B_h2T[tl])
                            if 1 <= ti <= nown:
                                kb.dma(A["y"][(ti - 1) * 128:ti * 128, :], xm[tl][:], [B_xm[tl]], [], B_xm[tl])
                    kb.barrier()
                    kb.release(p2bufs)

        with ExitStack() as st:
            h2w = [sbt(st, "h2w%d" % i, [128, NKC, 514], BF16) for i in range(2)]
            actT = sbt(st, "actT", [128, NFC, 512], BF16)
            wu = [sbt(st, "wu%d" % i, [128, NKC, 512], BF16) for i in range(3)]
            wd = [sbt(st, "wd%d" % i, [128, 4, 512], BF16) for i in range(6)]
            G = [sbt(st, "G%d" % i, [128, 514], F32) for i in range(2)]
            T1 = [sbt(st, "T1_%d" % i, [128, 512], F32) for i in range(2)]
            T2 = [sbt(st, "T2_%d" % i, [128, 512], F32) for i in range(2)]
            SS = [sbt(st, "SS%d" % i, [128, 512], F32) for i in range(2)]
            xo = [sbt(st, "xo%d" % i, [128, D], F32) for i in range(4)]
            m2 = sbt(st, "m2", [128, 2], F32)
            B_h2w = [Buf("h2w0"), Buf("h2w1")]; B_actT = Buf("actT"); B_wu = [Buf("wu%d" % i) for i in range(3)]
            B_wd = [Buf("wd%d" % i) for i in range(6)]; B_G = [Buf("G0"), Buf("G1")]; B_T1 = [Buf("T10"), Buf("T11")]
            B_T2 = [Buf("T20"), Buf("T21")]; B_SS = [Buf("SS0"), Buf("SS1")]; B_xo = [Buf("xo%d" % i) for i in range(4)]
            B_m2 = Buf("m2")
            groups = []
            for ph in phases:
                ng = ph["nown"] // 4
                for j in range(ng):
                    groups.append((ph, j, ng))

            def load_h2w(gi):
                ph, j, ng = groups[gi]
                c0 = 127 + 512 * j
                kb.dma(h2w[gi % 2][:], P[ph["name"]]["H2T"][:, :, c0:c0 + 514].rearrange("a p t -> p a t"), [], [B_h2w[gi % 2]], B_h2w[gi % 2])
            load_h2w(0)
            ucnt = 0
            dcnt = 0
            fcnt = 0
            upsrc = b_up.rearrange("(kc p) n -> p kc n", p=128)
            wu_next = [0]

            def wu_ensure(u_hi):
                u_hi = min(u_hi, len(groups) * 22 - 1)
                while wu_next[0] <= u_hi:
                    u = wu_next[0]
                    fq_, kind = (u % 22) // 2, u % 2
                    c0 = kind * DFF + fq_ * 512
                    kb.dma(wu[u % 3][:], upsrc[:, :, c0:c0 + 512], [], [B_wu[u % 3]], B_wu[u % 3], extra=conv_ev["up"])
                    wu_next[0] += 1
            for gi, (ph, j, ng) in enumerate(groups):
                A = P[ph["name"]]
                hb = gi % 2
                kb.op(dve, [], [B_m2], lambda: nc.vector.memset(m2[:], 1.0))
                if j == 0:
                    kb.op(dve, CONST + [B_m2], [B_m2], lambda: nc.vector.tensor_copy(m2[:, 0:1], cmask[:, 0:1]))
                if j == ng - 1:
                    kb.op(dve, CONST + [B_m2], [B_m2], lambda: nc.vector.tensor_copy(m2[:, 1:2], cmask[:, 1:2]))
                for tl in range(4):
                    r0 = (4 * j + tl) * 128
                    kb.dma(xo[tl][:], A["y"][r0:r0 + 128, :], [], [B_xo[tl]], B_xo[tl])
                for fq in range(11):
                    gs = (gi * 22 + 2 * fq) % 3
                    vs = (gi * 22 + 2 * fq + 1) % 3
                    wu_ensure(gi * 22 + 2 * fq + 1)
                    for f4 in range(4):
                        fc = fq * 4 + f4
                        r = fcnt % 2; fcnt += 1
                        gbk, vbk = r, 2 + r

                        def mm_up(sl, bk, edge):
                            def f():
                                for kc in range(NKC):
                                    ins = nc.tensor.matmul(bank(bk), lhsT=wu[sl][:, kc, f4 * 128:(f4 + 1) * 128], rhs=h2w[hb][:, kc, 1:513], start=(kc == 0), stop=(kc == NKC - 1))
                                if edge:
                                    for kc in range(NKC):
                                        ins = nc.tensor.matmul(ps[:, 4, r * 2:r * 2 + 2], lhsT=wu[sl][:, kc, f4 * 128:(f4 + 1) * 128],
                                                               rhs=V(h2w[hb], kc * 514, [(513, 2)]), start=(kc == 0), stop=(kc == NKC - 1))
                                return ins
                            return f
                        kb.op(pe, [B_wu[gs], B_h2w[hb]], [bankB[gbk], bankB[4]], mm_up(gs, gbk, True))
                        kb.op(pe, [B_wu[vs], B_h2w[hb]], [bankB[vbk]], mm_up(vs, vbk, False))
                        w0 = wconv[:, fc * 3:fc * 3 + 1]; w1 = wconv[:, fc * 3 + 1:fc * 3 + 2]; w2 = wconv[:, fc * 3 + 2:fc * 3 + 3]
                        bb = bconv[:, fc:fc + 1]
                        kb.op(act, [bankB[gbk]], [B_G[r]], lambda: nc.scalar.copy(out=G[r][:, 1:513], in_=bank(gbk)))
                        kb.op(dve, [bankB[4], B_m2, B_G[r]], [B_G[r]], lambda: nc.vector.tensor_tensor(
                            out=V(G[r], 0, [(513, 2)]), in0=ps[:, 4, r * 2:r * 2 + 2], in1=m2[:], op=ALU.mult))
                        kb.op(dve, [B_G[r]] + CONST, [B_T1[r]], lambda: nc.vector.tensor_scalar(
                            out=T1[r][:], in0=G[r][:, 1:513], scalar1=w1, scalar2=bb, op0=ALU.mult, op1=ALU.add))
                        kb.op(dve, [B_G[r], B_T1[r]] + CONST, [B_T2[r]], lambda: nc.vector.scalar_tensor_tensor(
                            out=T2[r][:], in0=G[r][:, 0:512], scalar=w0, in1=T1[r][:], op0=ALU.mult, op1=ALU.add))
                        kb.op(dve, [B_G[r], B_T2[r]] + CONST, [B_T1[r]], lambda: nc.vector.scalar_tensor_tensor(
                            out=T1[r][:], in0=G[r][:, 2:514], scalar=w2, in1=T2[r][:], op0=ALU.mult, op1=ALU.add))
                        kb.op(act, [B_T1[r]], [B_SS[r]], lambda: nc.scalar.activation(out=SS[r][:], in_=T1[r][:], func=AF.Silu))
                        kb.op(dve, [B_SS[r], bankB[vbk]], [B_actT], lambda: nc.vector.tensor_tensor(
                            out=actT[:, fc, :], in0=bank(vbk), in1=SS[r][:], op=ALU.mult))
                if gi + 1 < len(groups):
                    load_h2w(gi + 1)
                    wu_ensure(gi * 22 + 24)
                for nn in range(4):
                    for ku in range(11):
                        ds = dcnt % 6; dcnt += 1
                        kb.dma(wd[ds][:], b_down[ku * 512:(ku + 1) * 512, nn * 512:(nn + 1) * 512].rearrange("(a p) n -> p a n", p=128),
                               [], [B_wd[ds]], B_wd[ds], extra=conv_ev["down"])

                        def mm_dn(ds=ds, ku=ku, nn=nn):
                            for tl in range(4):
                                for kk in range(4):
                                    ins = nc.tensor.matmul(bank((nn % 2) * 4 + tl), lhsT=actT[:, ku * 4 + kk, tl * 128:(tl + 1) * 128], rhs=wd[ds][:, kk, :],
                                                           start=(ku == 0 and kk == 0), stop=(ku == 10 and kk == 3))
                            return ins
                        kb.op(pe, [B_actT, B_wd[ds]], [bankB[(nn % 2) * 4 + tl] for tl in range(4)], mm_dn)
                    for tl in range(4):
                        bk = (nn % 2) * 4 + tl
                        kb.op(dve, [bankB[bk], B_xo[tl]], [B_xo[tl]], lambda bk=bk, tl=tl, nn=nn: nc.vector.tensor_tensor(
                            out=xo[tl][:, nn * 512:(nn + 1) * 512], in0=bank(bk), in1=xo[tl][:, nn * 512:(nn + 1) * 512], op=ALU.add))
                for tl in range(4):
                    r0 = (4 * j + tl) * 128
                    kb.dma(A["y"][r0:r0 + 128, :], xo[tl][:], [B_xo[tl]], [], B_xo[tl])
            kb.barrier()
    return nc


def _tile_lists(T, half):
    nown = T // 2
    own = [half * nown + j for j in range(nown)]
    PH = -1
    hb = own[0] - 1 if half == 1 else PH
    ha = own[-1] + 1 if half == 0 else PH
    ql = [hb] + own + [ha]
    rest = [t for t in range(T) if t not in ql]
    return ql + rest, nown


def _rope_table(L):
    rows = L // GRID_W
    row = np.repeat(np.arange(rows, dtype=np.float32), GRID_W)
    col = np.tile(np.arange(GRID_W, dtype=np.float32), rows)
    inv = (np.float32(10000.0) ** (-np.arange(0, 64, 2, dtype=np.float32) / np.float32(64))).astype(np.float32)
    ar = (row[:, None] * inv[None, :]).astype(np.float32)
    ac = (col[:, None] * inv[None, :]).astype(np.float32)
    return np.concatenate([np.cos(ar), np.cos(ac), np.sin(ar), np.sin(ac)], axis=1).astype(np.float32)


def _bmats(half):
    out = np.zeros((20, 128, 128), np.float32)
    s = np.arange(128)[:, None]
    d = np.arange(128)[None, :]
    for g, w in enumerate(POOL_WINDOWS):
        h = w // 2
        inwin = lambda srcpos: ((srcpos >= d - h) & (srcpos <= d + h - 1)).astype(np.float32)
        out[g] = inwin(s - 128) / w
        out[4 + g] = inwin(s + 128) / w
        eye = (s == d).astype(np.float32)
        cint = inwin(s) / w - eye
        cnt_f = ((d + h) - np.maximum(d - h, 0)).astype(np.float32)
        cfirst = inwin(s) / cnt_f - eye
        cnt_l = (np.minimum(d + h, 128) - (d - h)).astype(np.float32)
        clast = inwin(s) / cnt_l - eye
        out[8 + g] = cfirst if half == 0 else cint
        out[12 + g] = cint
        out[16 + g] = clast if half == 1 else cint
    return np.ascontiguousarray(out.transpose(1, 0, 2).reshape(128, 20 * 128))


def _core_inputs(c, xs_seq, xp_seq, common, Ts, Tp, rope_s, rope_p):
    half = c % 2
    m = dict(common)
    for name, xseq, T, rope in (("s", xs_seq, Ts, rope_s), ("p", xp_seq, Tp, rope_p)):
        kl, nown = _tile_lists(T, half)
        NT = T + 1
        x = np.zeros((NT * 128, D), np.float32)
        rp = np.zeros((NT * 128, 128), np.float32)
        kbm = np.zeros((128, NT), np.float32)
        for i, t in enumerate(kl):
            if t < 0:
                kbm[:, i] = NEG
                rp[i * 128:(i + 1) * 128, 0:64] = 1.0
            else:
                x[i * 128:(i + 1) * 128] = xseq[t * 128:(t + 1) * 128]
                rp[i * 128:(i + 1) * 128] = rope[t * 128:(t + 1) * 128]
        m["x_" + name] = x
        m["rope_" + name] = rp
        m["kb_" + name] = kbm
    m["cmask"] = np.tile(np.array([[1.0 if half == 1 else 0.0, 1.0 if half == 0 else 0.0]], np.float32), (128, 1))
    m["bmat"] = _bmats(half)
    return m


def _common(g_norm_mix, w_in, g_q, g_k, w_pool, pool_scale, w_out, g_norm_ffn, w_up, w_conv, b_conv, w_down):
    f = lambda a: np.ascontiguousarray(np.asarray(a, np.float32))
    return dict(
        w_in=f(w_in[0]), w_out=f(w_out[0]), w_up=f(w_up[0]), w_down=f(w_down[0]), w_pool=f(np.asarray(w_pool[0]).reshape(1024, 256)),
        gmix=f(np.asarray(g_norm_mix[0]).reshape(NKC, 128).T), gffn=f(np.asarray(g_norm_ffn[0]).reshape(NKC, 128).T),
        gq=f(np.tile(np.asarray(g_q[0])[None, :], (128, 1))), gk=f(np.tile(np.asarray(g_k[0])[None, :], (128, 1))),
        pscale=f(np.asarray(pool_scale[0]).reshape(8, 128).T),
        wconv=f(np.asarray(w_conv[0]).reshape(3, NFC, 128).transpose(2, 1, 0).reshape(128, NFC * 3)),
        bconv=f(np.asarray(b_conv[0]).reshape(NFC, 128).T),
    )


def run(x_prompt, x_sample, weights, ncores=8, trace=False, debug=False):
    x_prompt = np.asarray(x_prompt, np.float32); x_sample = np.asarray(x_sample, np.float32)
    Bp, Lp, _ = x_prompt.shape
    Bs, Ls, _ = x_sample.shape
    Tp, Ts = Lp // 128, Ls // 128
    cfg = dict(phases=[dict(name="s", T=Ts, nown=Ts // 2), dict(name="p", T=Tp, nown=Tp // 2)], debug=debug)
    nc = build(cfg)
    common = _common(**weights)
    rope_s, rope_p = _rope_table(Ls), _rope_table(Lp)
    in_maps = [_core_inputs(c, x_sample[c // 2], x_prompt[c // 2], common, Ts, Tp, rope_s, rope_p) for c in range(ncores)]
    res = run_bass_kernel_spmd(nc, in_maps, core_ids=list(range(ncores)), **({"trace": True} if trace else {}))
    yp = np.zeros((Bp, Lp, D), np.float32); ys = np.zeros((Bs, Ls, D), np.float32)
    for c in range(ncores):
        s, half = c // 2, c % 2
        r = res.results[c]
        ys[s, half * (Ls // 2):(half + 1) * (Ls // 2)] = r["y_s"]
        yp[s, half * (Lp // 2):(half + 1) * (Lp // 2)] = r["y_p"]
    return (yp, ys), res


def kernel(x_prompt, x_sample, g_norm_mix, w_in, g_q, g_k, w_pool, pool_scale, w_out,
           g_norm_ffn, w_up, w_conv, b_conv, w_down):
    weights = dict(g_norm_mix=g_norm_mix, w_in=w_in, g_q=g_q, g_k=g_k, w_pool=w_pool, pool_scale=pool_scale, w_out=w_out,
                   g_norm_ffn=g_norm_ffn, w_up=w_up, w_conv=w_conv, b_conv=b_conv, w_down=w_down)
    (yp, ys), _ = run(x_prompt, x_sample, weights)
    return (yp, ys)
```
